# Optimizing a Trainium2 kernel written in Bass

```python
import math
import jax, jax.numpy as jnp
from jax import lax
import numpy as np

D_MODEL = 1024
BATCH = 16
SEQ = 4096
DEPTH = 2

N_META = 16
N_A_LAYERS = DEPTH // 2
N_B_LAYERS = DEPTH - N_A_LAYERS
D_FF = 2816
SSM_WIDTH = D_MODEL // 2
SSM_GROUP = 16
SSM_GROUPS = SSM_WIDTH // SSM_GROUP
SSM_STATE = 64
STEP_MIN = 1e-3
STEP_MAX = 1e-1
HEAD_DIM = 64
N_Q_HEADS = D_MODEL // HEAD_DIM
N_KV_HEADS = 4
Q_PER_KV = N_Q_HEADS // N_KV_HEADS
WINDOW = 128
BLOCK = 128
ROPE_THETA = 10000.0
EPS = 1e-6
NEG_INF = -1e30

kernel_name = "yoco_s5_swa_sink_macaron"


def rms_norm(x, g):
    xf = x.astype(jnp.float32)
    y = xf * lax.rsqrt(jnp.mean(xf * xf, axis=-1, keepdims=True) + EPS)
    return (y * g.astype(jnp.float32)).astype(x.dtype)


def rope(x, pos):
    half = HEAD_DIM // 2
    freqs = ROPE_THETA ** (-jnp.arange(0, half, dtype=jnp.float32) * 2.0 / HEAD_DIM)
    ang = pos.astype(jnp.float32)[:, None] * freqs[None, :]
    bshape = (pos.shape[0],) + (1,) * (x.ndim - 3) + (half,)
    cos = jnp.cos(ang).reshape(bshape)
    sin = jnp.sin(ang).reshape(bshape)
    xf = x.astype(jnp.float32)
    x1, x2 = xf[..., :half], xf[..., half:]
    return jnp.concatenate([x1 * cos - x2 * sin, x2 * cos + x1 * sin], axis=-1).astype(x.dtype)


def swiglu_ffn(h, g, w_gate_up, w_down):
    a, b = jnp.split(rms_norm(h, g) @ w_gate_up, 2, axis=-1)
    return (jax.nn.silu(a) * b) @ w_down


def _complex_scan_op(e1, e2):
    a1r, a1i, b1r, b1i = e1
    a2r, a2i, b2r, b2i = e2
    return (a2r * a1r - a2i * a1i,
            a2r * a1i + a2i * a1r,
            a2r * b1r - a2i * b1i + b2r,
            a2r * b1i + a2i * b1r + b2i)


def s5_mixer(hn, w_in, lam_re, lam_im, b_re, b_im, c_re, c_im, log_step, d_skip, w_out):
    bsz, L, _ = hn.shape
    f = lambda t: t.astype(jnp.float32)
    u = f(hn @ w_in)
    ug = u.reshape(bsz, L, SSM_GROUPS, SSM_GROUP)
    lr, li = f(lam_re), f(lam_im)
    step = jnp.exp(f(log_step))[:, None]
    mag = jnp.exp(lr * step)
    ar = mag * jnp.cos(li * step)
    ai = mag * jnp.sin(li * step)
    den = lr * lr + li * li
    nr, ni = ar - 1.0, ai
    cr = (nr * lr + ni * li) / den
    ci = (ni * lr - nr * li) / den
    br, bi = f(b_re), f(b_im)
    bbar_r = cr[..., None] * br - ci[..., None] * bi
    bbar_i = cr[..., None] * bi + ci[..., None] * br
    bu_r = jnp.einsum('blgc,gpc->blgp', ug, bbar_r)
    bu_i = jnp.einsum('blgc,gpc->blgp', ug, bbar_i)
    a_r = jnp.broadcast_to(ar, (1, L, SSM_GROUPS, SSM_STATE))
    a_i = jnp.broadcast_to(ai, (1, L, SSM_GROUPS, SSM_STATE))
    _, _, xr, xi = lax.associative_scan(_complex_scan_op, (a_r, a_i, bu_r, bu_i), axis=1)
    y = jnp.einsum('blgp,gcp->blgc', xr, f(c_re)) - jnp.einsum('blgp,gcp->blgc', xi, f(c_im))
    y = y.reshape(bsz, L, SSM_WIDTH) + f(d_skip) * u
    z = jax.nn.gelu(y).astype(hn.dtype) @ w_out
    a, g = jnp.split(z, 2, axis=-1)
    return a * jax.nn.sigmoid(g)


def shared_kv(h, g_kv, w_kv, k_gain):
    bsz, L, _ = h.shape
    k, v = jnp.split(rms_norm(h, g_kv) @ w_kv, 2, axis=-1)
    k = k.reshape(bsz, L, N_KV_HEADS, HEAD_DIM)
    v = v.reshape(bsz, L, N_KV_HEADS, HEAD_DIM)
    k = rope(rms_norm(k, k_gain), jnp.arange(L))
    return k, v


def swa_sink_attention(hn, k, v, w_q, q_gain, sinks, w_o):
    bsz, S, _ = hn.shape
    nb = S // BLOCK
    q = (hn @ w_q).reshape(bsz, S, N_KV_HEADS, Q_PER_KV, HEAD_DIM)
    q = rope(rms_norm(q, q_gain), N_META + jnp.arange(S))
    qb = q.reshape(bsz, nb, BLOCK, N_KV_HEADS, Q_PER_KV, HEAD_DIM)
    k_meta, v_meta = k[:, :N_META], v[:, :N_META]
    k_blk = k[:, N_META:].reshape(bsz, nb, BLOCK, N_KV_HEADS, HEAD_DIM)
    v_blk = v[:, N_META:].reshape(bsz, nb, BLOCK, N_KV_HEADS, HEAD_DIM)
    pad = ((0, 0), (1, 0), (0, 0), (0, 0), (0, 0))
    k_band = jnp.concatenate([jnp.pad(k_blk, pad)[:, :-1], k_blk], axis=2)
    v_band = jnp.concatenate([jnp.pad(v_blk, pad)[:, :-1], v_blk], axis=2)
    scale = HEAD_DIM ** -0.5
    s_band = jnp.einsum('bnqhgd,bnkhd->bnhgqk', qb, k_band,
                        preferred_element_type=jnp.float32) * scale
    qi = jnp.arange(BLOCK)[:, None]
    kj = jnp.arange(2 * BLOCK)[None, :]
    rel = qi + BLOCK - kj
    blk = jnp.arange(nb)[:, None, None]
    valid = (rel >= 0) & (rel < WINDOW) & ((blk > 0) | (kj >= BLOCK))
    s_band = jnp.where(valid[None, :, None, None], s_band, NEG_INF)
    s_meta = jnp.einsum('bnqhgd,bmhd->bnhgqm', qb, k_meta,
                        preferred_element_type=jnp.float32) * scale
    sink = sinks.astype(jnp.float32).reshape(N_KV_HEADS, Q_PER_KV)[None, None, :, :, None]
    m = jnp.maximum(jnp.maximum(s_band.max(-1), s_meta.max(-1)), sink)
    p_band = jnp.exp(s_band - m[..., None])
    p_meta = jnp.exp(s_meta - m[..., None])
    denom = p_band.sum(-1) + p_meta.sum(-1) + jnp.exp(sink - m)
    o = (jnp.einsum('bnhgqk,bnkhd->bnqhgd', p_band, v_band.astype(jnp.float32))
         + jnp.einsum('bnhgqm,bmhd->bnqhgd', p_meta, v_meta.astype(jnp.float32)))
    o = o / jnp.moveaxis(denom, -1, 2)[..., None]
    return o.reshape(bsz, S, N_Q_HEADS * HEAD_DIM).astype(hn.dtype) @ w_o


def setup_inputs(seed: int = 0) -> dict:
    key = jax.random.key(seed)
    ks = jax.random.split(key, 32)
    f32 = jnp.float32

    def nrm(k, shape, scale):
        return jax.random.normal(k, shape, f32) * scale

    H, G, P, C = SSM_WIDTH, SSM_GROUPS, SSM_STATE, SSM_GROUP
    return {
        "x": nrm(ks[0], (BATCH, SEQ, D_MODEL), 1.0),
        "meta_tokens": nrm(ks[1], (N_META, D_MODEL), 1.0),
        "ffn1_norm": 1.0 + nrm(ks[2], (DEPTH, D_MODEL), 0.02),
        "ffn1_w_gate_up": nrm(ks[3], (DEPTH, D_MODEL, 2 * D_FF), D_MODEL ** -0.5),
        "ffn1_w_down": nrm(ks[4], (DEPTH, D_FF, D_MODEL), D_FF ** -0.5),
        "mix_norm": 1.0 + nrm(ks[5], (DEPTH, D_MODEL), 0.02),
        "ffn2_norm": 1.0 + nrm(ks[6], (DEPTH, D_MODEL), 0.02),
        "ffn2_w_gate_up": nrm(ks[7], (DEPTH, D_MODEL, 2 * D_FF), D_MODEL ** -0.5),
        "ffn2_w_down": nrm(ks[8], (DEPTH, D_FF, D_MODEL), D_FF ** -0.5),
        "ssm_w_in": nrm(ks[9], (N_A_LAYERS, D_MODEL, H), D_MODEL ** -0.5),
        "ssm_lambda_re": -0.5 + nrm(ks[10], (N_A_LAYERS, G, P), 0.01),
        "ssm_lambda_im": jnp.pi * jnp.arange(P, dtype=f32) + nrm(ks[11], (N_A_LAYERS, G, P), 0.01),
        "ssm_b_re": nrm(ks[12], (N_A_LAYERS, G, P, C), (2 * C) ** -0.5),
        "ssm_b_im": nrm(ks[13], (N_A_LAYERS, G, P, C), (2 * C) ** -0.5),
        "ssm_c_re": nrm(ks[14], (N_A_LAYERS, G, C, P), P ** -0.5),
        "ssm_c_im": nrm(ks[15], (N_A_LAYERS, G, C, P), P ** -0.5),
        "ssm_log_step": jax.random.uniform(ks[16], (N_A_LAYERS, G), f32,
                                           minval=math.log(STEP_MIN), maxval=math.log(STEP_MAX)),
        "ssm_d": nrm(ks[17], (N_A_LAYERS, H), 1.0),
        "ssm_w_out": nrm(ks[18], (N_A_LAYERS, H, 2 * D_MODEL), H ** -0.5),
        "kv_norm": 1.0 + nrm(ks[19], (D_MODEL,), 0.02),
        "w_kv": nrm(ks[20], (D_MODEL, 2 * N_KV_HEADS * HEAD_DIM), D_MODEL ** -0.5),
        "k_norm": 1.0 + nrm(ks[21], (HEAD_DIM,), 0.02),
        "attn_w_q": nrm(ks[22], (N_B_LAYERS, D_MODEL, N_Q_HEADS * HEAD_DIM), D_MODEL ** -0.5),
        "q_norm": 1.0 + nrm(ks[23], (N_B_LAYERS, HEAD_DIM), 0.02),
        "attn_sinks": nrm(ks[24], (N_B_LAYERS, N_Q_HEADS), 0.5),
        "attn_w_o": nrm(ks[25], (N_B_LAYERS, N_Q_HEADS * HEAD_DIM, D_MODEL), (N_Q_HEADS * HEAD_DIM) ** -0.5),
    }


def reference(x, meta_tokens, ffn1_norm, ffn1_w_gate_up, ffn1_w_down, mix_norm, ffn2_norm,
              ffn2_w_gate_up, ffn2_w_down, ssm_w_in, ssm_lambda_re, ssm_lambda_im, ssm_b_re,
              ssm_b_im, ssm_c_re, ssm_c_im, ssm_log_step, ssm_d, ssm_w_out, kv_norm, w_kv,
              k_norm, attn_w_q, q_norm, attn_sinks, attn_w_o):
    bsz = x.shape[0]
    meta = jnp.broadcast_to(meta_tokens.astype(x.dtype)[None], (bsz, N_META, D_MODEL))
    h = jnp.concatenate([meta, x], axis=1)
    k = v = None
    for layer in range(DEPTH):
        if layer == N_A_LAYERS:
            k, v = shared_kv(h, kv_norm, w_kv, k_norm)
            h = h[:, N_META:]
        h = h + 0.5 * swiglu_ffn(h, ffn1_norm[layer], ffn1_w_gate_up[layer], ffn1_w_down[layer])
        hn = rms_norm(h, mix_norm[layer])
        if layer < N_A_LAYERS:
            h = h + s5_mixer(hn, ssm_w_in[layer], ssm_lambda_re[layer], ssm_lambda_im[layer],
                             ssm_b_re[layer], ssm_b_im[layer], ssm_c_re[layer], ssm_c_im[layer],
                             ssm_log_step[layer], ssm_d[layer], ssm_w_out[layer])
        else:
            j = layer - N_A_LAYERS
            h = h + swa_sink_attention(hn, k, v, attn_w_q[j], q_norm[j], attn_sinks[j], attn_w_o[j])
        h = h + 0.5 * swiglu_ffn(h, ffn2_norm[layer], ffn2_w_gate_up[layer], ffn2_w_down[layer])
    return h
```

```python
import numpy as np
from contextlib import ExitStack
import concourse.bass as bass
import concourse.mybir as mybir
from concourse.bass_utils import run_bass_kernel_spmd

F32 = mybir.dt.float32
BF16 = mybir.dt.bfloat16
AF = mybir.ActivationFunctionType
ALU = mybir.AluOpType

D = 1024
DFF = 2816
NF = DFF // 128
KC = D // 128
NMETA = 16
EPS = 1e-6


class Buf:
    __slots__ = ("name", "writers", "readers", "slot")

    def __init__(self, name):
        self.name = name
        self.writers = []
        self.readers = []
        self.slot = None


class Op:
    __slots__ = ("eng", "emit", "deps", "dma", "sig", "sem", "val", "vc", "slotbuf", "idx")


class Prog:
    ENGS = ("pe", "act", "dve", "pool", "sp")

    def __init__(self, nc):
        self.nc = nc
        self.ops = []
        self.nbuf = 0
        self.fence = None
        self.fence_idx = 0
        self.capture = None
        self.captured = []

    def buf(self, name=None):
        self.nbuf += 1
        return Buf(name or f"b{self.nbuf}")

    def bufs(self, n, name="b"):
        return [self.buf(f"{name}{i}") for i in range(n)]

    def op(self, eng, emit, reads=(), writes=(), dma=False, slot=None):
        if self.capture:
            self.captured.append((eng, emit, list(reads), list(writes), dma, slot))
            return None
        o = Op()
        o.eng = eng
        o.emit = emit
        o.dma = dma
        o.sig = False
        o.sem = None
        o.val = 0
        o.vc = None
        o.slotbuf = slot
        o.idx = len(self.ops)
        deps = []
        if self.fence is not None:
            deps.append(self.fence)
        for b in reads:
            for w in b.writers:
                deps.append(w)
        for b in writes:
            for w in b.writers:
                deps.append(w)
            for r in b.readers:
                deps.append(r)
        fd = []
        for d in deps:
            if d is o:
                continue
            if (not d.dma) and (not dma) and d.eng == eng:
                if eng == "pe":
                    continue
            fd.append(d)
        o.deps = fd
        for d in fd:
            d.sig = True
        for b in reads:
            b.readers.append(o)
        for b in writes:
            b.writers = [o]
            b.readers = []
        self.ops.append(o)
        return o

    def splice(self, k=None):
        n = len(self.captured) if k is None else min(k, len(self.captured))
        todo, self.captured = self.captured[:n], self.captured[n:]
        cap, self.capture = self.capture, None
        for a in todo:
            self.op(*a)
        self.capture = cap

    def barrier(self):
        last = {}
        deps = []
        for o in self.ops[self.fence_idx:]:
            if o.dma:
                deps.append(o)
            elif o.eng != "sp":
                last[o.eng] = o
        deps += list(last.values())
        if self.fence is not None and not deps:
            return
        B = Op()
        B.eng = "sp"
        B.emit = lambda e: e.nop()
        B.dma = False
        B.sig = True
        B.sem = None
        B.val = 0
        B.vc = None
        B.slotbuf = None
        B.idx = len(self.ops)
        B.deps = deps
        for d in deps:
            d.sig = True
        self.ops.append(B)
        self.fence = B
        self.fence_idx = len(self.ops)

    def dma(self, eng, out, in_, reads, writes, slot):
        return self.op(eng, lambda e: e.dma_start(out=out, in_=in_), reads, writes, dma=True, slot=slot)

    def finalize(self, es, final_reads):
        nc = self.nc
        self.op("sp", None, reads=final_reads)
        esem = {e: es.enter_context(nc.semaphore(f"s_{e}")) for e in self.ENGS}
        ecount = {e: 0 for e in self.ENGS}
        slotsem = {}
        slotcount = {}
        waited = {e: {} for e in self.ENGS}
        streams = {e: [] for e in self.ENGS}
        nwait = 0
        for o in self.ops:
            E = o.eng
            wl = []
            wd = waited[E]
            need = {}
            for d in o.deps:
                k = id(d.sem)
                if wd.get(k, (None, 0))[1] >= d.val:
                    continue
                if k not in need or need[k][1] < d.val:
                    need[k] = (d.sem, d.val, d)
            for k, (sem, val, d) in need.items():
                if wd.get(k, (None, 0))[1] >= val:
                    continue
                wl.append((sem, val))
                for k2, (s2, v2) in d.vc.items():
                    if wd.get(k2, (None, 0))[1] < v2:
                        wd[k2] = (s2, v2)
                wd[k] = (sem, val)
            nwait += len(wl)
            if o.sig:
                if o.dma:
                    sb = o.slotbuf
                    if id(sb) not in slotsem:
                        slotsem[id(sb)] = es.enter_context(nc.semaphore(f"d_{len(slotsem)}"))
                        slotcount[id(sb)] = 0
                    slotcount[id(sb)] += 16
                    o.sem = slotsem[id(sb)]
                    o.val = slotcount[id(sb)]
                    o.vc = dict(wd)
                    o.vc[id(o.sem)] = (o.sem, o.val)
                else:
                    ecount[E] += 1
                    o.sem = esem[E]
                    o.val = ecount[E]
                    o.vc = dict(wd)
                    o.vc[id(o.sem)] = (o.sem, o.val)
            streams[E].append((wl, o))
        self.stats = dict(nops=len(self.ops), nwait=nwait, nsem=len(slotsem) + 5,
                          per_eng={e: len(streams[e]) for e in self.ENGS})
        block = es.enter_context(nc.Block())

        def run(engobj, lst):
            for wl, o in lst:
                for sem, val in wl:
                    engobj.wait_ge(sem, val)
                if o.emit is None:
                    continue
                ins = o.emit(engobj)
                if o.sig:
                    ins.then_inc(o.sem, 16 if o.dma else 1)

        @block.sync
        def _(e):
            run(e, streams["sp"])

        @block.tensor
        def _(e):
            run(e, streams["pe"])

        @block.scalar
        def _(e):
            run(e, streams["act"])

        @block.vector
        def _(e):
            run(e, streams["dve"])

        @block.gpsimd
        def _(e):
            run(e, streams["pool"])


import math

HD = 64
NG = 32
NP = 64
TB = 16
TWO_PI = 2.0 * math.pi
GELU_C = 2.0 * math.sqrt(2.0 / math.pi)


class Cfg:
    def __init__(self, seq=4096, nseq=2, upto=99):
        self.seq = seq
        self.nseq = nseq
        self.L = seq + NMETA
        self.upto = upto


class T:
    __slots__ = ("ap", "b")

    def __init__(self, ap, b):
        self.ap = ap
        self.b = b

    def __getitem__(self, k):
        return self.ap[k]


def build(cfg):
    nc = bass.Bass("TRN2", target_bir_lowering=False)
    SEQ, NSEQ, L = cfg.seq, cfg.nseq, cfg.L
    NTOK = NSEQ * SEQ
    LT = NSEQ * L
    NB = L // TB
    NTILE = SEQ // 512
    UPTO = cfg.upto

    def din(name, shape, dt=F32):
        return nc.dram_tensor(name, list(shape), dt, kind="ExternalInput").ap()

    def dscr(name, shape, dt):
        return nc.dram_tensor(name, list(shape), dt, kind="Internal").ap()

    xT = din("xT", [D, NTOK])
    metaT = din("metaT", [D, NMETA])
    w_gu = [[din(f"wgu{l}{i}", [D, 2 * DFF]) for i in range(2)] for l in range(2)]
    w_dn = [[din(f"wdn{l}{i}", [DFF, D]) for i in range(2)] for l in range(2)]
    gains = din("gains", [128, 7, KC])
    w_in = din("w_in", [D, 512])
    w_out = din("w_out", [512, 2 * D])
    w_kv = din("w_kv", [D, 512])
    w_q = din("w_q", [D, D])
    w_o = din("w_o", [D, D])
    lamS_d = din("lamS", [128, 3, 16])
    bcS_d = din("bcS", [128, 4, 256])
    lamC_d = din("lamC", [128, 3, 256])
    bC_d = din("bC", [128, 2, 256])
    dC_d = din("dC", [128, 4])
    qkg_d = din("qkg", [128, 2])
    sink_d = din("sinkL", [128, 8])
    rope_d = din("ropeT", [2, 128, L])
    rmat_d = din("rmat", [128, 128])
    outT = nc.dram_tensor("outT", [D, NTOK], F32, kind="ExternalOutput").ap()
    hA = dscr("hA", [D, LT], F32)
    uTs = dscr("uTs", [512, LT], BF16)
    ygs = dscr("ygs", [512, LT], BF16)
    KTs = dscr("KTs", [256, LT], BF16)
    Vs = dscr("Vs", [LT, 256], BF16)

    es = ExitStack()
    with es:
        P = Prog(nc)
        ARB = 210944
        AR = es.enter_context(nc.sbuf_tensor("AR", [128, ARB // 2], BF16))
        st = {"off": 0}

        def alloc(shape, dt, name=None, nb=1):
            esz = 4 if dt == F32 else (4 if dt == mybir.dt.int32 else 2)
            n = 1
            for x in shape[1:]:
                n *= x
            nbytes = (n * esz + 31) // 32 * 32
            o = st["off"]
            assert o + nbytes <= ARB, f"arena overflow {name} {o + nbytes}"
            st["off"] = o + nbytes
            v = AR[0:shape[0], o // 2:(o + n * esz) // 2]
            if dt != BF16:
                v = v.bitcast(dt)
            if len(shape) > 2:
                names = " ".join(f"a{i}" for i in range(len(shape) - 1))
                kw = {f"a{i}": shape[i + 1] for i in range(len(shape) - 1)}
                v = v.rearrange(f"p ({names}) -> p {names}", **kw)
            b = P.buf(name) if nb == 1 else P.bufs(nb, name or "t")
            return T(v, b)

        psall = es.enter_context(nc.psum_tensor("psall", [128, 4096], F32))
        ps = [psall[:, i * 512:(i + 1) * 512] for i in range(8)]
        b_ps = P.bufs(8, "ps")

        def bl(x):
            out = []
            for t in x:
                if isinstance(t, Buf):
                    out.append(t)
                elif isinstance(t.b, list):
                    out.extend(t.b)
                else:
                    out.append(t.b)
            return out

        def tt(eng, out, a, b, op, R, W):
            P.op(eng, lambda e: e.tensor_tensor(out=out, in0=a, in1=b, op=op), bl(R), bl(W))

        def ts(eng, out, a, s1, s2, op0, op1, R, W):
            if s2 is None:
                s2, op1 = 0.0, ALU.add
            P.op(eng, lambda e: e.tensor_scalar(out=out, in0=a, scalar1=s1, scalar2=s2, op0=op0, op1=op1), bl(R), bl(W))

        def stt(eng, out, a, sc, b, op0, op1, R, W):
            P.op(eng, lambda e: e.scalar_tensor_tensor(out=out, in0=a, scalar=sc, in1=b, op0=op0, op1=op1), bl(R), bl(W))

        def actf(out, a, func, R, W, scale=1.0, bias=0.0):
            P.op("act", lambda e: e.activation(out=out, in_=a, func=func, scale=scale, bias=bias), bl(R), bl(W))

        def cp(eng, out, a, R, W):
            if eng == "act":
                P.op("act", lambda e: e.activation(out=out, in_=a, func=AF.Copy), bl(R), bl(W))
            else:
                P.op(eng, lambda e: e.tensor_copy(out=out, in_=a), bl(R), bl(W))

        def mset(eng, out, val, W):
            P.op(eng, lambda e: e.memset(out, val), [], bl(W))

        def mm(out, lhsT, rhs, start, stop, R, W):
            P.op("pe", lambda e: e.matmul(out, lhsT=lhsT, rhs=rhs, start=start, stop=stop), bl(R), bl(W))

        def dma(eng, out, in_, R, W, slot):
            sl = slot.b if isinstance(slot, T) else slot
            if isinstance(sl, list):
                sl = sl[0]
            P.dma(eng, out, in_, bl(R), bl(W), sl)

        DBG = getattr(cfg, "dbg", False)
        dbg_bufs = []

        def dump(name, t, shape, dt):
            if not DBG:
                return
            dd = nc.dram_tensor("z_" + name, list(shape), dt, kind="ExternalOutput").ap()
            bo = P.buf("z" + name)
            src = t.ap if isinstance(t, T) else t
            P.dma("sp", dd, src, bl([t]) if isinstance(t, T) else [], [bo], bo)
            dbg_bufs.append(bo)

        ones = alloc([128, 128], BF16, "ones")
        gn = alloc([128, 7, KC], F32, "gn")
        rmat = alloc([128, 128], BF16, "rmat")
        qkg = alloc([128, 2], F32, "qkg")
        sinkx = alloc([128, 8], F32, "sinkx")
        dC = alloc([128, 4], F32, "dC")
        mset("pool", ones[:], 1.0, [ones])
        ones2 = alloc([128, 128], BF16, "ones2")
        mset("pool", ones2[:], 0.0, [ones2])
        mset("pool", ones2[0:64, 0:64], 1.0, [ones2])
        mset("pool", ones2[64:128, 64:128], 1.0, [ones2])
        dma("sp", gn[:], gains, [], [gn], gn)
        dma("sp", qkg[:], qkg_d, [], [qkg], qkg)
        dma("sp", sinkx[:], sink_d, [], [sinkx], sinkx)
        dma("sp", dC[:], dC_d, [], [dC], dC)
        dma("pool", rmat[:], rmat_d, [], [rmat], rmat)
        actf(sinkx[:], sinkx[:], AF.Exp, [sinkx], [sinkx])
        PERSIST = st["off"]

        xTv = xT.rearrange("(k p) t -> p k t", p=128)
        metaTv = metaT.rearrange("(k p) t -> p k t", p=128)
        hAv = hA.rearrange("(k p) t -> p k t", p=128)
        outTv = outT.rearrange("(k p) t -> p k t", p=128)
        uTv = uTs.rearrange("(c p) t -> p c t", p=128)
        ygv = ygs.rearrange("(c p) t -> p c t", p=128)
        tiles = [(0, NMETA)] + [(NMETA + 512 * i, 512) for i in range(NTILE)]
        qtiles = tiles[1:]
        b_hA = {(s, p0): P.buf(f"hA{s}_{p0}") for s in range(NSEQ) for (p0, n) in tiles}
        b_uT = {(s, p0): P.buf(f"uT{s}_{p0}") for s in range(NSEQ) for (p0, n) in tiles}
        b_yg = {s: P.buf(f"yg{s}") for s in range(NSEQ)}
        b_kv = {(s, p0): P.buf(f"kv{s}_{p0}") for s in range(NSEQ) for (p0, n) in tiles}
        b_out = []

        class FFNCtx:
            pass

        def ffn_alloc(with_weights=True, with_sg=True):
            c = FFNCtx()
            c.h = alloc([128, KC, 512], F32, "h", nb=KC)
            c.hn = alloc([128, KC, 512], BF16, "hn", nb=KC)
            c.sq = [alloc([128, 512], BF16, f"sq{i}") for i in range(2)]
            if with_sg:
                c.sg = [alloc([128, 512], F32, f"sg{i}") for i in range(2)]
            c.rstd = alloc([128, 512], F32, "rstd")
            if with_weights:
                c.Wgu = alloc([128, KC, 2 * DFF], BF16, "Wgu")
                c.Wdn = alloc([128, NF, D], BF16, "Wdn")
                c.act = alloc([128, NF, 512], BF16, "act", nb=NF)
            return c

        def load_ffn_weights(c, l, i):
            dma("pool", c.Wgu[:], w_gu[l][i].rearrange("(k p) f -> p k f", p=128), [], [c.Wgu], c.Wgu)
            dma("pool", c.Wdn[:], w_dn[l][i].rearrange("(f p) d -> p f d", p=128), [], [c.Wdn], c.Wdn)

        def rmsnorm(c, n, gi, h=None):
            h = c.h if h is None else h
            hn, sq, rstd = c.hn, c.sq, c.rstd
            for k in range(KC):
                s = k % 2
                actf(sq[s][:, :n], h[:, k, :n], AF.Square, [h.b[k]], [sq[s]])
                mm(ps[6][:, :n], ones[:], sq[s][:, :n], k == 0, k == KC - 1, [ones, sq[s]], [b_ps[6]])
            actf(rstd[:, :n], ps[6][:, :n], AF.Ln, [b_ps[6]], [rstd], scale=1.0 / D, bias=EPS)
            actf(rstd[:, :n], rstd[:, :n], AF.Exp, [rstd], [rstd], scale=-0.5)
            for k in range(KC):
                stt("dve", hn[:, k, :n], h[:, k, :n], gn[:, gi, k:k + 1], rstd[:, :n], ALU.mult, ALU.mult,
                    [h.b[k], gn, rstd], [hn.b[k]])

        def ffn_gateup(c, n):
            hn, sg, act, Wgu = c.hn, c.sg, c.act, c.Wgu
            for f in range(NF):
                pa, pb = 2 * (f % 2), 2 * (f % 2) + 1
                for k in range(KC):
                    mm(ps[pa][:, :n], Wgu[:, k, f * 128:(f + 1) * 128], hn[:, k, :n], k == 0, k == KC - 1,
                       [Wgu, hn.b[k]], [b_ps[pa]])
                for k in range(KC):
                    mm(ps[pb][:, :n], Wgu[:, k, DFF + f * 128:DFF + (f + 1) * 128], hn[:, k, :n], k == 0, k == KC - 1,
                       [Wgu, hn.b[k]], [b_ps[pb]])
                s = f % 2
                actf(sg[s][:, :n], ps[pa][:, :n], AF.Silu, [b_ps[pa]], [sg[s]])
                tt("dve", act[:, f, :n], sg[s][:, :n], ps[pb][:, :n], ALU.mult, [sg[s], b_ps[pb]], [act.b[f]])

        def ffn_down(c, n, h=None):
            h = c.h if h is None else h
            act, Wdn = c.act, c.Wdn
            for k in range(KC):
                pd = 4 + (k % 2)
                for f in range(NF):
                    mm(ps[pd][:, :n], Wdn[:, f, k * 128:(k + 1) * 128], act[:, f, :n], f == 0, f == NF - 1,
                       [Wdn, act.b[f]], [b_ps[pd]])
                stt("dve", h[:, k, :n], ps[pd][:, :n], 0.5, h[:, k, :n], ALU.mult, ALU.add,
                    [b_ps[pd], h.b[k]], [h.b[k]])

        def ffn(c, n, gi):
            rmsnorm(c, n, gi)
            ffn_gateup(c, n)
            ffn_down(c, n)

        def ffn_phase_pipelined(c, gi, mode):
            hb = [c.h, alloc([128, KC, 512], F32, "h2", nb=KC)]
            tls = tiles if mode in ("in", "mid") else qtiles
            tl = [(s, p0, n) for s in range(NSEQ) for (p0, n) in tls]

            def ld(i):
                s, p0, n = tl[i]
                h = hb[i % 2]
                if mode == "in":
                    if p0 == 0:
                        dma("sp", h[:, :, :n], metaTv, [], [h], h.b[0])
                    else:
                        c0 = s * SEQ + p0 - NMETA
                        dma("sp", h[:, :, :n], xTv[:, :, c0:c0 + n], [], [h], h.b[0])
                else:
                    dma("sp", h[:, :, :n], hAv[:, :, s * L + p0:s * L + p0 + n], [b_hA[(s, p0)]], [h], h.b[0])
                rmsnorm(c, n, gi, h=h)

            ld(0)
            for i, (s, p0, n) in enumerate(tl):
                h = hb[i % 2]
                ffn_gateup(c, n)
                if i + 1 < len(tl):
                    ld(i + 1)
                ffn_down(c, n, h=h)
                if mode == "out":
                    c0 = s * SEQ + p0 - NMETA
                    bo = P.buf("o")
                    dma("pool", outTv[:, :, c0:c0 + n], h[:, :, :n], [h], [bo], h.b[1])
                    b_out.append(bo)
                else:
                    dma("pool", hAv[:, :, s * L + p0:s * L + p0 + n], h[:, :, :n], [h], [b_hA[(s, p0)]], h.b[1])

        def load_h(c, s, p0, n):
            dma("sp", c.h[:, :, :n], hAv[:, :, s * L + p0:s * L + p0 + n], [b_hA[(s, p0)]], [c.h], c.h.b[0])

        def store_h(c, s, p0, n):
            dma("pool", hAv[:, :, s * L + p0:s * L + p0 + n], c.h[:, :, :n], [c.h], [b_hA[(s, p0)]], c.h.b[1])

        def store_out(c, s, p0, n):
            c0 = s * SEQ + p0 - NMETA
            bo = P.buf("o")
            dma("sp", outTv[:, :, c0:c0 + n], c.h[:, :, :n], [c.h], [bo], c.h.b[1])
            b_out.append(bo)

        def phase_begin():
            P.barrier()
            st["off"] = PERSIST

        phase_begin()
        c = ffn_alloc()
        load_ffn_weights(c, 0, 0)
        if UPTO == 1:
            for s in range(NSEQ):
                for (p0, n) in tiles:
                    if p0 == 0:
                        dma("sp", c.h[:, :, :n], metaTv, [], [c.h], c.h.b[0])
                    else:
                        c0 = s * SEQ + p0 - NMETA
                        dma("sp", c.h[:, :, :n], xTv[:, :, c0:c0 + n], [], [c.h], c.h.b[0])
                    ffn(c, n, 0)
                    if p0 > 0:
                        store_out(c, s, p0, n)
        else:
            ffn_phase_pipelined(c, 0, "in")
            phase_begin()
            P.capture = True
            def lam_setup(src_d, F, tag):
                lam = alloc([128, 3, F], F32, "lam" + tag)
                dma("sp", lam[:], src_d, [], [lam], lam)
                r = {}
                for nm in ("step", "a", "th", "ar", "ai", "cr", "ci", "t1", "t2", "t3", "kk"):
                    r[nm] = alloc([128, F], F32, nm + tag)
                r["Pr"] = alloc([128, 17, F], F32, "Pr" + tag)
                r["Pi"] = alloc([128, 17, F], F32, "Pi" + tag)
                lr, li, ls = lam[:, 0, :], lam[:, 1, :], lam[:, 2, :]
                actf(r["step"][:], ls, AF.Exp, [lam], [r["step"]])
                tt("dve", r["a"][:], lr, r["step"][:], ALU.mult, [lam, r["step"]], [r["a"]])
                tt("dve", r["th"][:], li, r["step"][:], ALU.mult, [lam, r["step"]], [r["th"]])
                actf(r["t3"][:], r["a"][:], AF.Exp, [r["a"]], [r["t3"]])
                for (dst, shift) in (("ai", math.pi), ("ar", 1.5 * math.pi)):
                    ts("dve", r["t1"][:], r["th"][:], shift, None, ALU.add, None, [r["th"]], [r["t1"]])
                    mset("dve", r["kk"][:], 0.0, [r["kk"]])
                    for m in range(1, 17):
                        stt("dve", r["kk"][:], r["t1"][:], TWO_PI * m, r["kk"][:], ALU.is_ge, ALU.add,
                            [r["t1"], r["kk"]], [r["kk"]])
                    stt("dve", r["t2"][:], r["kk"][:], -TWO_PI, r["t1"][:], ALU.mult, ALU.add,
                        [r["kk"], r["t1"]], [r["t2"]])
                    ts("dve", r["t2"][:], r["t2"][:], -math.pi, None, ALU.add, None, [r["t2"]], [r["t2"]])
                    actf(r[dst][:], r["t2"][:], AF.Sin, [r["t2"]], [r[dst]])
                    tt("dve", r[dst][:], r[dst][:], r["t3"][:], ALU.mult, [r[dst], r["t3"]], [r[dst]])
                tt("dve", r["t1"][:], lr, lr, ALU.mult, [lam], [r["t1"]])
                tt("dve", r["t2"][:], li, li, ALU.mult, [lam], [r["t2"]])
                tt("dve", r["t1"][:], r["t1"][:], r["t2"][:], ALU.add, [r["t1"], r["t2"]], [r["t1"]])
                P.op("dve", lambda e: e.reciprocal(out=r["t1"][:], in_=r["t1"][:]), bl([r["t1"]]), bl([r["t1"]]))
                ts("dve", r["t3"][:], r["ar"][:], -1.0, None, ALU.add, None, [r["ar"]], [r["t3"]])
                tt("dve", r["t2"][:], r["t3"][:], lr, ALU.mult, [r["t3"], lam], [r["t2"]])
                tt("dve", r["kk"][:], r["ai"][:], li, ALU.mult, [r["ai"], lam], [r["kk"]])
                tt("dve", r["t2"][:], r["t2"][:], r["kk"][:], ALU.add, [r["t2"], r["kk"]], [r["t2"]])
                tt("dve", r["cr"][:], r["t2"][:], r["t1"][:], ALU.mult, [r["t2"], r["t1"]], [r["cr"]])
                tt("dve", r["t2"][:], r["ai"][:], lr, ALU.mult, [r["ai"], lam], [r["t2"]])
                tt("dve", r["kk"][:], r["t3"][:], li, ALU.mult, [r["t3"], lam], [r["kk"]])
                tt("dve", r["t2"][:], r["t2"][:], r["kk"][:], ALU.subtract, [r["t2"], r["kk"]], [r["t2"]])
                tt("dve", r["ci"][:], r["t2"][:], r["t1"][:], ALU.mult, [r["t2"], r["t1"]], [r["ci"]])
                Pr, Pi = r["Pr"], r["Pi"]
                mset("dve", Pr[:, 0, :], 1.0, [Pr])
                mset("dve", Pi[:, 0, :], 0.0, [Pi])
                for k in range(1, 17):
                    cmul("dve", Pr[:, k, :], Pi[:, k, :], Pr[:, k - 1, :], Pi[:, k - 1, :], r["ar"][:], r["ai"][:],
                         r["t1"], r["t2"], [Pr, Pi, r["ar"], r["ai"]], [Pr, Pi])
                return r

            def cmul(eng, o_re, o_im, a_re, a_im, b_re, b_im, t1, t2, R, W, neg_im=False):
                sh = tuple(a_re.shape)
                v1, v2 = t1.ap, t2.ap
                if tuple(v1.shape) != sh:
                    n = 1
                    for x in sh[1:]:
                        n *= x
                    v1 = t1.ap[0:sh[0], 0:n]
                    v2 = t2.ap[0:sh[0], 0:n]
                    if len(sh) == 3:
                        v1 = v1.rearrange("p (a b) -> p a b", b=sh[2])
                        v2 = v2.rearrange("p (a b) -> p a b", b=sh[2])
                    elif len(sh) == 4:
                        v1 = v1.rearrange("p (a b c) -> p a b c", b=sh[2], c=sh[3])
                        v2 = v2.rearrange("p (a b c) -> p a b c", b=sh[2], c=sh[3])
                tt(eng, v1, a_re, b_re, ALU.mult, R, [t1])
                tt(eng, v2, a_im, b_im, ALU.mult, R, [t2])
                tt(eng, o_re, v1, v2, ALU.subtract, [t1, t2], W)
                tt(eng, v1, a_re, b_im, ALU.mult, R, [t1])
                tt(eng, v2, a_im, b_re, ALU.mult, R, [t2])
                if neg_im:
                    stt(eng, o_im, v1, -1.0, v2, ALU.mult, ALU.subtract, [t1, t2], W)
                else:
                    tt(eng, o_im, v1, v2, ALU.add, [t1, t2], W)

            VB = alloc([128, 2, 16, 16], BF16, "VB")
            VR = alloc([128, 17, 2, 16, 16], BF16, "VR")
            VI = alloc([128, TB, 2, 256], BF16, "VI")
            r16 = alloc([128, 16], F32, "r16")
            Tr = alloc([128, 16, NB], F32, "Tr")
            Ti = alloc([128, 16, NB], F32, "Ti")
            maskS = alloc([128, 4, 4, 2], BF16, "maskS")
            maskC = alloc([128, 4, 2], BF16, "maskC")
            SSM_KEEP = st["off"]
            rS = lam_setup(lamS_d, 16, "S")
            rC = lam_setup(lamC_d, 256, "C")
            tmpA = alloc([128, 512], F32, "tmpA")
            tmpB = alloc([128, 512], F32, "tmpB")
            bcS = alloc([128, 4, 16, 16], F32, "bcS")
            dma("sp", bcS[:], bcS_d.rearrange("p a (g c) -> p a g c", c=16), [], [bcS], bcS)
            bc16 = lambda ap: ap.unsqueeze(2).to_broadcast([128, 16, 16])
            VBf = alloc([128, 2, 16, 16], F32, "VBf")
            cmul("dve", VBf[:, 0], VBf[:, 1], bc16(rS["cr"][:]), bc16(rS["ci"][:]), bcS[:, 0], bcS[:, 1], tmpA, tmpB,
                 [rS["cr"], rS["ci"], bcS], [VBf])
            cp("dve", VB[:], VBf[:], [VBf], [VB])
            for k in range(17):
                cmul("dve", VR[:, k, 0], VR[:, k, 1], bcS[:, 2], bcS[:, 3], bc16(rS["Pr"][:, k, :]), bc16(rS["Pi"][:, k, :]),
                     tmpA, tmpB, [bcS, rS["Pr"], rS["Pi"]], [VR], neg_im=True)
            bC = alloc([128, 2, 256], F32, "bC")
            dma("sp", bC[:], bC_d, [], [bC], bC)
            VBC = alloc([128, 2, 256], F32, "VBC")
            cmul("dve", VBC[:, 0], VBC[:, 1], rC["cr"][:], rC["ci"][:], bC[:, 0], bC[:, 1], tmpA, tmpB,
                 [rC["cr"], rC["ci"], bC], [VBC])
            for j in range(TB):
                k = TB - 1 - j
                cmul("pool", VI[:, j, 0], VI[:, j, 1], rC["Pr"][:, k, :], rC["Pi"][:, k, :], VBC[:, 0], VBC[:, 1],
                     tmpA, tmpB, [rC["Pr"], rC["Pi"], VBC], [VI])
            actf(r16[:], rS["a"][:], AF.Exp, [rS["a"]], [r16], scale=float(TB))
            ir16 = alloc([128, 16], F32, "ir16")
            P.op("dve", lambda e: e.reciprocal(out=ir16[:], in_=r16[:]), bl([r16]), bl([ir16]))
            cur = alloc([128, 2, 16], F32, "cur")
            cur2 = alloc([128, 2, 16], F32, "cur2")
            tt("dve", cur[:, 0, :], rS["Pr"][:, 16, :], ir16[:], ALU.mult, [rS["Pr"], ir16], [cur])
            tt("dve", cur[:, 1, :], rS["Pi"][:, 16, :], ir16[:], ALU.mult, [rS["Pi"], ir16], [cur])
            mset("dve", Tr[:, :, 0:1], 1.0, [Tr])
            mset("dve", Ti[:, :, 0:1], 0.0, [Ti])
            w = 1
            a_, b_ = cur, cur2
            while w < NB:
                hi = min(2 * w, NB)
                cnt = hi - w
                gstep = max(1, min(16, 512 // cnt))
                for g0 in range(0, 16, gstep):
                    g1 = min(16, g0 + gstep)
                    bcw = lambda ap, cnt=cnt, g0=g0, g1=g1: ap.unsqueeze(2).to_broadcast([128, g1 - g0, cnt])
                    cmul("dve", Tr[:, g0:g1, w:hi], Ti[:, g0:g1, w:hi], Tr[:, g0:g1, 0:cnt], Ti[:, g0:g1, 0:cnt],
                         bcw(a_[:, 0, g0:g1]), bcw(a_[:, 1, g0:g1]), tmpA, tmpB, [Tr, Ti, a_], [Tr, Ti])
                cmul("dve", b_[:, 0, :], b_[:, 1, :], a_[:, 0, :], a_[:, 1, :], a_[:, 0, :], a_[:, 1, :], tmpA, tmpB,
                     [a_], [b_])
                a_, b_ = b_, a_
                w *= 2
            mset("pool", maskS[:], 0.0, [maskS])
            for g in range(4):
                mset("pool", maskS[0:64, g, g, 0:1], 1.0, [maskS])
                mset("pool", maskS[64:128, g, g, 1:2], 1.0, [maskS])
            iot = alloc([128, 4, 2], mybir.dt.int32, "iot")
            iof = alloc([128, 4, 2], F32, "iof")
            iog = alloc([128, 4, 2], F32, "iog")
            P.op("pool", lambda e: e.iota(iot[:], pattern=[[-32, 4], [-16, 2]], base=0, channel_multiplier=1), [], bl([iot]))
            cp("dve", iof[:], iot[:], [iot], [iof])
            ts("dve", iog[:], iof[:], 0.0, None, ALU.is_ge, None, [iof], [iog])
            ts("dve", iof[:], iof[:], 16.0, None, ALU.is_lt, None, [iof], [iof])
            tt("dve", maskC[:], iof[:], iog[:], ALU.mult, [iof, iog], [maskC])

            for nm in ("ar", "ai", "cr", "ci", "th", "a"):
                dump("S_" + nm, rS[nm], [128, 16], F32)
            dump("S_Pr", rS["Pr"], [128, 17, 16], F32)
            dump("S_Pi", rS["Pi"], [128, 17, 16], F32)
            dump("C_Pr", rC["Pr"], [128, 17, 256], F32)
            dump("VB", VB, [128, 2, 16, 16], BF16)
            dump("VR", VR, [128, 17, 2, 16, 16], BF16)
            dump("VI", VI, [128, TB, 2, 256], BF16)
            dump("r16", r16, [128, 16], F32)
            dump("Tr", Tr, [128, 16, NB], F32)
            dump("Ti", Ti, [128, 16, NB], F32)
            dump("maskS", maskS, [128, 4, 4, 2], BF16)
            dump("maskC", maskC, [128, 4, 2], BF16)
            P.capture = False
            n_setup = len(P.captured)
            cs1 = [ffn_alloc(with_weights=False, with_sg=False) for _ in range(2)]
            Win = alloc([128, KC, 512], BF16, "Win")
            ubs = [alloc([128, 4, 512], BF16, f"ub{i}") for i in range(2)]
            dma("pool", Win[:], w_in.rearrange("(k p) f -> p k f", p=128), [], [Win], Win)
            tl1 = [(s, p0, n) for s in range(NSEQ) for (p0, n) in tiles]
            load_h(cs1[0], tl1[0][0], tl1[0][1], tl1[0][2])
            rmsnorm(cs1[0], tl1[0][2], 4)
            for i, (s, p0, n) in enumerate(tl1):
                c = cs1[i % 2]
                ub = ubs[i % 2]
                if i + 1 < len(tl1):
                    load_h(cs1[(i + 1) % 2], tl1[i + 1][0], tl1[i + 1][1], tl1[i + 1][2])
                for ct in range(4):
                    for k in range(KC):
                        mm(ps[ct][:, :n], Win[:, k, ct * 128:(ct + 1) * 128], c.hn[:, k, :n], k == 0, k == KC - 1,
                           [Win, c.hn.b[k]], [b_ps[ct]])
                if i + 1 < len(tl1):
                    rmsnorm(cs1[(i + 1) % 2], tl1[i + 1][2], 4)
                for ct in range(4):
                    cp("act" if ct % 2 == 0 else "pool" if False else "dve", ub[:, ct, :n], ps[ct][:, :n], [b_ps[ct]], [ub])
                dma("pool", uTv[:, :, s * L + p0:s * L + p0 + n], ub[:, :, :n], [ub], [b_uT[(s, p0)]], ub)
                P.splice((n_setup + len(tl1) - 1) // len(tl1))
            P.splice()

        if UPTO >= 2:
            P.barrier()
            st["off"] = SSM_KEEP
            tmpA = alloc([128, 4, NB], F32, "tmpA2")
            tmpB = alloc([128, 4, NB], F32, "tmpB2")
            sqt = [alloc([128, NB], F32, "sqt0")] * 2
            Wread = alloc([128, 4, 17, 2, 128], BF16, "Wread", nb=34)
            XB = alloc([128, 4, 2, 128], BF16, "XB")
            Winj = alloc([128, 4, TB, 2, 128], BF16, "Winj", nb=2 * TB)
            Kconv = alloc([128, TB, 128], BF16, "Kconv")
            uT = [alloc([128, L], BF16, f"uT{i}") for i in range(2)]
            uD = alloc([128, TB, NB], BF16, "uD", nb=2)
            yg = [alloc([128, L], BF16, "ygt0")] * 2
            mr = alloc([128, 4, NB], F32, "mr")
            mi = alloc([128, 4, NB], F32, "mi")
            wr = alloc([128, 4, NB], F32, "wr")
            wi = alloc([128, 4, NB], F32, "wi")
            Xp = alloc([128, 4, 2, NB + 1], BF16, "Xp")
            ytmp = [alloc([128, NB], F32, f"ytmp{i}") for i in range(2)]
            gt = [alloc([128, NB], F32, f"gt{i}") for i in range(2)]
            mset("pool", Xp[:], 0.0, [Xp])
            print("P2a arena", st["off"], flush=True)
            it = 0
            for ct in range(4):
                m8 = maskS[:].rearrange("p g a b -> p g (a b)").unsqueeze(3).to_broadcast([128, 4, 8, 16])
                for k in range(17):
                    for ri in range(2):
                        eng = "pool" if (k + ri) % 2 else "dve"
                        tt(eng, Wread[:, :, k, ri, :].rearrange("p g (m c) -> p g m c", c=16), m8,
                           VR[:, k, ri, 4 * ct:4 * ct + 4, :].unsqueeze(2).to_broadcast([128, 4, 8, 16]), ALU.mult,
                           [maskS, VR], [Wread.b[2 * k + ri]])
                for ri in range(2):
                    tt("dve", XB[:, :, ri, :].rearrange("p g (m c) -> p g m c", c=16), m8,
                       VB[:, ri, 4 * ct:4 * ct + 4, :].unsqueeze(2).to_broadcast([128, 4, 8, 16]), ALU.mult,
                       [maskS, VB], [XB])
                mC = maskC[:].unsqueeze(3).to_broadcast([128, 4, 2, 64])
                for j in range(TB):
                    for ri in range(2):
                        eng = "pool" if (j + ri) % 2 else "dve"
                        tt(eng, Winj[:, :, j, ri, :].rearrange("p g (a q) -> p g a q", q=64), mC,
                           VI[:, j, ri, ct * 64:(ct + 1) * 64].unsqueeze(1).unsqueeze(1).to_broadcast([128, 4, 2, 64]),
                           ALU.mult, [maskC, VI], [Winj.b[2 * j + ri]])
                for t4 in range(TB // 4):
                    pb = t4 % 2
                    for tq in range(4):
                        tau = t4 * 4 + tq
                        i = 0
                        for g in range(4):
                            for ri in range(2):
                                mm(ps[pb][:, tq * 128:(tq + 1) * 128], XB[:, g, ri, :], Wread[:, g, tau, ri, :], i == 0, i == 7,
                                   [XB, Wread.b[2 * tau + ri]], [b_ps[pb]])
                                i += 1
                    cp("act", Kconv[:, t4 * 4:(t4 + 1) * 4, :].rearrange("p a b -> p (a b)"), ps[pb][:], [b_ps[pb]], [Kconv])
                if ct == 0:
                    dump("Wread", Wread, [128, 4, 17, 2, 128], BF16)
                    dump("Winj", Winj, [128, 4, TB, 2, 128], BF16)
                    dump("XB", XB, [128, 4, 2, 128], BF16)
                    dump("Kconv", Kconv, [128, TB, 128], BF16)
                for s in range(NSEQ):
                    u_, y_ = uT[it % 2], yg[it % 2]
                    if it == 0:
                        dma("sp", u_[:], uTv[:, ct, s * L:(s + 1) * L], [b_uT[(s, p0)] for (p0, n) in tiles], [u_], u_)
                    it += 1
                    if it < 4 * NSEQ:
                        ctn, sn_ = it // NSEQ, it % NSEQ
                        un = uT[it % 2]
                        dma("sp", un[:], uTv[:, ctn, sn_ * L:(sn_ + 1) * L], [b_uT[(sn_, p0)] for (p0, n) in tiles], [un], un)
                    ub_ = u_[:].rearrange("p (b j) -> p b j", j=TB)
                    yb_ = y_[:].rearrange("p (b j) -> p b j", j=TB)
                    cp("act", uD[:, 0:TB // 2, :], u_[:].rearrange("p (b j) -> p j b", j=TB)[:, 0:TB // 2, :], [u_], [uD.b[0]])
                    cp("pool", uD[:, TB // 2:TB, :], u_[:].rearrange("p (b j) -> p j b", j=TB)[:, TB // 2:TB, :], [u_], [uD.b[1]])
                    for g in range(4):
                        for ri in range(2):
                            pp = g * 2 + ri
                            for j in range(TB):
                                mm(ps[pp][:, :NB], Winj[:, g, j, ri, :], uD[:, j, :], j == 0, j == TB - 1,
                                   [Winj.b[2 * j + ri], uD], [b_ps[pp]])
                    cc, ss = Tr[:, 4 * ct:4 * ct + 4, :], Ti[:, 4 * ct:4 * ct + 4, :]
                    pv = psall[:].rearrange("p (g r c) -> p g r c", r=2, c=512)
                    ir_, ii_ = pv[:, :, 0, 0:NB], pv[:, :, 1, 0:NB]
                    bre = [b_ps[0], b_ps[2], b_ps[4], b_ps[6]]
                    bim = [b_ps[1], b_ps[3], b_ps[5], b_ps[7]]
                    ta, tb = tmpA[:], tmpB[:]
                    tt("dve", ta, cc, ir_, ALU.mult, [Tr] + bre, [tmpA])
                    tt("dve", tb, ss, ii_, ALU.mult, [Ti] + bim, [tmpB])
                    tt("dve", mr[:], ta, tb, ALU.add, [tmpA, tmpB], [mr])
                    tt("dve", ta, cc, ii_, ALU.mult, [Tr] + bim, [tmpA])
                    tt("dve", tb, ss, ir_, ALU.mult, [Ti] + bre, [tmpB])
                    tt("dve", mi[:], ta, tb, ALU.subtract, [tmpA, tmpB], [mi])
                    for g in range(4):
                        rb = r16[:, 4 * ct + g:4 * ct + g + 1].to_broadcast([128, NB])
                        P.op("dve", lambda e, g=g, rb=rb: e.tensor_tensor_scan(out=wr[:, g, :], data0=rb, data1=mr[:, g, :],
                                                                              initial=0.0, op0=ALU.mult, op1=ALU.add),
                             bl([r16, mr]), bl([wr]))
                        P.op("dve", lambda e, g=g, rb=rb: e.tensor_tensor_scan(out=wi[:, g, :], data0=rb, data1=mi[:, g, :],
                                                                              initial=0.0, op0=ALU.mult, op1=ALU.add),
                             bl([r16, mi]), bl([wi]))
                    tt("dve", ta, cc, wr[:], ALU.mult, [Tr, wr], [tmpA])
                    tt("dve", tb, ss, wi[:], ALU.mult, [Ti, wi], [tmpB])
                    tt("dve", Xp[:, :, 0, 1:NB + 1], ta, tb, ALU.subtract, [tmpA, tmpB], [Xp])
                    tt("pool", mr[:], cc, wi[:], ALU.mult, [Tr, wi], [mr])
                    tt("pool", mi[:], ss, wr[:], ALU.mult, [Ti, wr], [mi])
                    tt("pool", Xp[:, :, 1, 1:NB + 1], mr[:], mi[:], ALU.add, [mr, mi], [Xp])
                    if ct == 0 and s == 0:
                        dump("mr", mr, [128, 4, NB], F32)
                        dump("mi", mi, [128, 4, NB], F32)
                        dump("wr", wr, [128, 4, NB], F32)
                        dump("Xp", Xp, [128, 4, 2, NB + 1], BF16)
                    for j in range(TB):
                        pp = j % 2
                        nmm = 8 + j + 1
                        i = 0
                        for g in range(4):
                            for ri in range(2):
                                mm(ps[pp][:, :NB], Wread[:, g, j + 1, ri, :], Xp[:, g, ri, 0:NB], i == 0, i == nmm - 1,
                                   [Wread.b[2 * (j + 1) + ri], Xp], [b_ps[pp]])
                                i += 1
                        for tau in range(j + 1):
                            mm(ps[pp][:, :NB], Kconv[:, tau, :], uD[:, j - tau, :], i == 0, i == nmm - 1,
                               [Kconv, uD], [b_ps[pp]])
                            i += 1
                        yt, g_, sq_ = ytmp[pp], gt[pp], sqt[pp]
                        stt("dve", yt[:], uD[:, j, :], dC[:, ct:ct + 1], ps[pp][:, :NB], ALU.mult, ALU.add,
                            [uD, dC, b_ps[pp]], [yt])
                        actf(sq_[:], yt[:], AF.Square, [yt], [sq_])
                        ts("pool", g_[:], sq_[:], 0.044715, 1.0, ALU.mult, ALU.add, [sq_], [g_])
                        tt("pool", g_[:], g_[:], yt[:], ALU.mult, [g_, yt], [g_])
                        actf(g_[:], g_[:], AF.Sigmoid, [g_], [g_], scale=GELU_C)
                        tt("dve", yb_[:, :, j], yt[:], g_[:], ALU.mult, [yt, g_], [y_])
                    dma("sp", ygv[:, ct, s * L:(s + 1) * L], y_[:], [y_], [b_yg[s]], y_)

        if UPTO >= 3:
            phase_begin()
            cs2 = [ffn_alloc(with_weights=False) for _ in range(2)]
            Wout = alloc([128, 4, 2 * D], BF16, "Wout")
            ygb = [alloc([128, 4, 512], BF16, f"ygb{i}") for i in range(2)]
            dma("pool", Wout[:], w_out.rearrange("(c p) f -> p c f", p=128), [], [Wout], Wout)
            it = 0
            for s in range(NSEQ):
                for (p0, n) in tiles:
                    yb = ygb[it % 2]
                    c = cs2[it % 2]
                    it += 1
                    load_h(c, s, p0, n)
                    dma("sp", yb[:, :, :n], ygv[:, :, s * L + p0:s * L + p0 + n], [b_yg[s]], [yb], yb)
                    for k in range(KC):
                        pa, pg = 2 * (k % 2), 2 * (k % 2) + 1
                        for ct in range(4):
                            mm(ps[pa][:, :n], Wout[:, ct, k * 128:(k + 1) * 128], yb[:, ct, :n], ct == 0, ct == 3,
                               [Wout, yb], [b_ps[pa]])
                        for ct in range(4):
                            mm(ps[pg][:, :n], Wout[:, ct, D + k * 128:D + (k + 1) * 128], yb[:, ct, :n], ct == 0, ct == 3,
                               [Wout, yb], [b_ps[pg]])
                        sgk = c.sg[k % 2]
                        actf(sgk[:, :n], ps[pg][:, :n], AF.Sigmoid, [b_ps[pg]], [sgk])
                        tt("dve", sgk[:, :n], sgk[:, :n], ps[pa][:, :n], ALU.mult, [sgk, b_ps[pa]], [sgk])
                        tt("pool", c.h[:, k, :n], c.h[:, k, :n], sgk[:, :n], ALU.add, [c.h.b[k], sgk], [c.h.b[k]])
                    store_h(c, s, p0, n)

        def qk_part1(n, src_ps, b_src, gcol, tset, psA):
            sqh, qh, rs, t1, t2 = tset
            actf(sqh[:, :n], src_ps[:, :n], AF.Square, [b_src], [sqh])
            mm(ps[psA][:, :n], ones2[:], sqh[:, :n], True, True, [ones2, sqh], [b_ps[psA]])
            actf(rs[:, :n], ps[psA][:, :n], AF.Ln, [b_ps[psA]], [rs], scale=1.0 / HD, bias=EPS)
            actf(rs[:, :n], rs[:, :n], AF.Exp, [rs], [rs], scale=-0.5)
            stt("dve", t1[:, :n], src_ps[:, :n], gcol, rs[:, :n], ALU.mult, ALU.mult, [b_src, qkg, rs], [t1])
            cp("act", qh[:, :n], t1[:, :n], [t1], [qh])

        def qk_part2(n, cs, dst, W, tset, psB):
            sqh, qh, rs, t1, t2 = tset
            mm(ps[psB][:, :n], rmat[:], qh[:, :n], True, True, [rmat, qh], [b_ps[psB]])
            tt("pool", t1[:, :n], t1[:, :n], cs[:, 0, :n], ALU.mult, [t1, cs], [t1])
            tt("dve", t2[:, :n], ps[psB][:, :n], cs[:, 1, :n], ALU.mult, [b_ps[psB], cs], [t2])
            tt("pool", dst, t1[:, :n], t2[:, :n], ALU.add, [t1, t2], W)

        def qk_norm_rope(n, src_ps, b_src, gcol, cs, sn, dst, W, tq, tr_, psA, psB):
            tset = (tq[0], tq[1], tr_[0], tr_[1], tr_[2])
            qk_part1(n, src_ps, b_src, gcol, tset, psA)
            qk_part2(n, cs, dst, W, tset, psB)

        ropev = rope_d.rearrange("c p t -> p c t")
        KTv = KTs.rearrange("(j p) t -> p j t", p=128)

        if UPTO >= 4:
            phase_begin()
            c = ffn_alloc()
            load_ffn_weights(c, 0, 1)
            ffn_phase_pipelined(c, 1, "mid")
            phase_begin()
            cs2c = [ffn_alloc(with_weights=False) for _ in range(2)]
            Wkv = alloc([128, KC, 512], BF16, "Wkv")
            dma("pool", Wkv[:], w_kv.rearrange("(k p) f -> p k f", p=128), [], [Wkv], Wkv)
            csb2 = [alloc([128, 2, 512], F32, f"csk{i}") for i in range(2)]
            ktps = [alloc([128, 2, 512], BF16, f"ktp{i}") for i in range(2)]
            vtps = [alloc([128, 4, 256], BF16, f"vtp{i}") for i in range(2)]
            tsk = [(alloc([128, 512], BF16, f"ksq{i}"), alloc([128, 512], BF16, f"kqh{i}"), alloc([128, 512], F32, f"krs{i}"),
                    alloc([128, 512], F32, f"kt1{i}"), alloc([128, 512], F32, f"kt2{i}")) for i in range(2)]
            tl2 = [(s, p0, n) for s in range(NSEQ) for (p0, n) in tiles]

            def pre2(i):
                s, p0, n = tl2[i]
                load_h(cs2c[i % 2], s, p0, n)
                dma("sp", csb2[i % 2][:, :, :n], ropev[:, :, p0:p0 + n], [], [csb2[i % 2]], csb2[i % 2])

            pre2(0)
            rmsnorm(cs2c[0], tl2[0][2], 6)
            for i, (s, p0, n) in enumerate(tl2):
                c = cs2c[i % 2]
                cs, ktp, vtp = csb2[i % 2], ktps[i % 2], vtps[i % 2]
                PBk = (0, 1) if i % 2 == 0 else (2, 3)
                if i + 1 < len(tl2):
                    pre2(i + 1)
                for j in range(2):
                    for k in range(KC):
                        mm(ps[PBk[j]][:, :n], Wkv[:, k, j * 128:(j + 1) * 128], c.hn[:, k, :n], k == 0, k == KC - 1,
                           [Wkv, c.hn.b[k]], [b_ps[PBk[j]]])
                nb_ = (n + 127) // 128
                for bi in range(nb_):
                    m = min(128, n - bi * 128)
                    pv_ = 4 + (bi % 2)
                    for k in range(KC):
                        mm(ps[pv_][:m, 0:256], c.hn[:, k, bi * 128:bi * 128 + m], Wkv[:, k, 256:512], k == 0, k == KC - 1,
                           [Wkv, c.hn.b[k]], [b_ps[pv_]])
                    cp("act" if bi % 2 == 0 else "dve", vtp[:m, bi, :], ps[pv_][:m, 0:256], [b_ps[pv_]], [vtp])
                if i + 1 < len(tl2):
                    rmsnorm(cs2c[(i + 1) % 2], tl2[i + 1][2], 6)
                for j in range(2):
                    qk_part1(n, ps[PBk[j]], b_ps[PBk[j]], qkg[:, 1:2], tsk[j], 7)
                for j in range(2):
                    qk_part2(n, cs, ktp[:, j, :n], [ktp], tsk[j], PBk[j])
                dma("pool", KTv[:, :, s * L + p0:s * L + p0 + n], ktp[:, :, :n], [ktp], [b_kv[(s, p0)]], ktp)
                r0 = s * L + p0
                if n == 512:
                    dma("pool", Vs[r0:r0 + n, :].rearrange("(b p) f -> p b f", p=128), vtp[:, :, :], [vtp], [b_kv[(s, p0)]], vtp)
                else:
                    dma("pool", Vs[r0:r0 + n, :], vtp[:n, 0, :], [vtp], [b_kv[(s, p0)]], vtp)

        if UPTO >= 5:
            phase_begin()
            c = ffn_alloc()
            load_ffn_weights(c, 1, 0)
            ffn_phase_pipelined(c, 2, "q")

        if UPTO >= 6:
            phase_begin()
            cs4 = [ffn_alloc(with_weights=False) for _ in range(2)]
            Wq = alloc([128, KC, D], BF16, "Wq")
            Wo = alloc([128, KC, D], BF16, "Wo")
            dma("pool", Wq[:], w_q.rearrange("(k p) f -> p k f", p=128), [], [Wq], Wq)
            dma("pool", Wo[:], w_o.rearrange("(k p) f -> p k f", p=128), [], [Wo], Wo)
            csb = [alloc([128, 2, 512], F32, f"cs{i}") for i in range(2)]
            tsets = [(alloc([128, 512], BF16, f"sqh{i}"), alloc([128, 512], BF16, f"qh{i}"), alloc([128, 512], F32, f"rs{i}"),
                      alloc([128, 512], F32, f"t1{i}"), alloc([128, 512], F32, f"t2{i}")) for i in range(2)]
            Qp = alloc([128, 8, 512], BF16, "Qp", nb=8)
            oT = alloc([128, 8, 512], BF16, "oT", nb=8)
            Kdb = [alloc([128, 4, 640], BF16, f"Kd{i}") for i in range(2)]
            Kmb = [alloc([128, 4, NMETA], BF16, f"Km{i}") for i in range(2)]
            Vlob = [alloc([128, 5, 4, 128], BF16, f"Vlo{i}", nb=5) for i in range(2)]
            Vhib = [alloc([128, 5, 4, 128], BF16, f"Vhi{i}", nb=5) for i in range(2)]
            Vmlob = [alloc([NMETA, 4, 128], BF16, f"Vmlo{i}") for i in range(2)]
            Vmhib = [alloc([NMETA, 4, 128], BF16, f"Vmhi{i}") for i in range(2)]
            olo = alloc([128, 128], BF16, "olo")
            ohi = alloc([128, 128], BF16, "ohi")
            PTp = [alloc([128, 512], BF16, f"PTp{i}") for i in range(2)]
            PTc = [alloc([128, 512], BF16, f"PTc{i}") for i in range(2)]
            PTm = [alloc([128, 512], BF16, f"PTm{i}") for i in range(2)]
            mask2 = alloc([128, 2, 128], BF16, "mask2")
            rden = [alloc([128, 512], F32, f"rden{i}") for i in range(2)]
            for t_ in Vlob + Vhib + Vmlob + Vmhib + [olo, ohi]:
                mset("pool", t_[:], 0.0, [t_])
            mset("pool", olo[:, 0:64], 1.0, [olo])
            mset("pool", ohi[:, 64:128], 1.0, [ohi])
            mi32 = alloc([128, 128], mybir.dt.int32, "mi32")
            mf32 = alloc([128, 128], F32, "mf32")
            P.op("pool", lambda e: e.iota(mi32[:], pattern=[[1, 128]], base=0, channel_multiplier=-1), [], bl([mi32]))
            cp("dve", mf32[:], mi32[:], [mi32], [mf32])
            ts("dve", mask2[:, 1, :], mf32[:], 0.0, None, ALU.is_ge, None, [mf32], [mask2])
            ts("dve", mask2[:, 0, :], mf32[:], 0.0, None, ALU.is_lt, None, [mf32], [mask2])
            scale = HD ** -0.5
            SCB = (0, 1)
            SMB = (2, 3)
            OB = (4, 6)
            DB = (5, 7)
            tl4 = [(s, ti, p0, n) for s in range(NSEQ) for ti, (p0, n) in enumerate(qtiles)]

            def pre_dma(i):
                s, ti, p0, n = tl4[i]
                c = cs4[i % 2]
                if ti == 0:
                    Km, Vmlo, Vmhi = Kmb[s % 2], Vmlob[s % 2], Vmhib[s % 2]
                    for hh in range(2):
                        dma("sp", Km[64 * hh:64 * hh + 64, :, :], KTs[:, s * L:s * L + NMETA].rearrange("(h d) t -> d h t", d=64),
                            [b_kv[(s, 0)]], [Km], Km)
                    dma("sp", Vmlo[:, :, 0:64], Vs[s * L:s * L + NMETA, :].rearrange("t (h d) -> t h d", d=64), [b_kv[(s, 0)]], [Vmlo], Vmlo)
                    dma("sp", Vmhi[:, :, 64:128], Vs[s * L:s * L + NMETA, :].rearrange("t (h d) -> t h d", d=64), [b_kv[(s, 0)]], [Vmhi], Vmhi)
                load_h(c, s, p0, n)
                cs = csb[i % 2]
                dma("sp", cs[:, :, :n], ropev[:, :, p0:p0 + n], [], [cs], cs)
                Kd, Vlo, Vhi = Kdb[i % 2], Vlob[i % 2], Vhib[i % 2]
                k0 = p0 - 128 if ti > 0 else p0
                nk = p0 + n - k0
                ko = 0 if ti > 0 else 1
                kvdeps = [b_kv[(s, p0)]] + ([b_kv[(s, qtiles[ti - 1][0])]] if ti > 0 else [])
                for hh in range(2):
                    dma("sp", Kd[64 * hh:64 * hh + 64, :, ko * 128:ko * 128 + nk],
                        KTs[:, s * L + k0:s * L + k0 + nk].rearrange("(h d) t -> d h t", d=64), kvdeps, [Kd], Kd)
                for bi in range(nk // 128):
                    vsrc = Vs[s * L + k0 + bi * 128:s * L + k0 + (bi + 1) * 128, :].rearrange("p (h d) -> p h d", d=64)
                    dma("sp", Vlo[:, ko + bi, :, 0:64], vsrc, kvdeps, [Vlo.b[ko + bi]], Vlo.b[ko + bi])
                    dma("sp", Vhi[:, ko + bi, :, 64:128], vsrc, kvdeps, [Vhi.b[ko + bi]], Vhi.b[ko + bi])

            def pre_norm(i):
                s, ti, p0, n = tl4[i]
                rmsnorm(cs4[i % 2], n, 5)

            pre_dma(0)
            pre_norm(0)
            for it4, (s, ti, p0, n) in enumerate(tl4):
                if True:
                    c = cs4[it4 % 2]
                    cs = csb[it4 % 2]
                    Kd, Vlo, Vhi = Kdb[it4 % 2], Vlob[it4 % 2], Vhib[it4 % 2]
                    Km, Vmlo, Vmhi = Kmb[s % 2], Vmlob[s % 2], Vmhib[s % 2]
                    if it4 + 1 < len(tl4):
                        pre_dma(it4 + 1)
                    QB = (0, 1, 2)
                    SB_ = (3, 4)
                    RB = (5, 6)
                    for jj in range(8 + 2):
                        if jj < 8:
                            j = jj
                            pq = QB[j % 3]
                            for k in range(KC):
                                mm(ps[pq][:, :n], Wq[:, k, j * 128:(j + 1) * 128], c.hn[:, k, :n], k == 0, k == KC - 1,
                                   [Wq, c.hn.b[k]], [b_ps[pq]])
                        if 0 <= jj - 1 < 8:
                            j = jj - 1
                            qk_part1(n, ps[QB[j % 3]], b_ps[QB[j % 3]], qkg[:, 0:1], tsets[j % 2], SB_[j % 2])
                        if 0 <= jj - 2 < 8:
                            j = jj - 2
                            qk_part2(n, cs, Qp[:, j, :n], [Qp.b[j]], tsets[j % 2], RB[j % 2])
                    if it4 + 1 < len(tl4):
                        pre_norm(it4 + 1)
                    steps = [(kvh, blk) for kvh in range(4) for blk in range(4)]
                    PB = (0, 3)
                    CB = (1, 4)
                    MB = (2, 5)
                    OB = (6, 7)

                    def emit_scores(i):
                        kvh, blk = steps[i]
                        pb_, cb_, mb_ = PB[i % 2], CB[i % 2], MB[i % 2]
                        qs = slice(blk * 128, (blk + 1) * 128)
                        has_prev = not (ti == 0 and blk == 0)
                        qb = [Qp.b[2 * kvh], Qp.b[2 * kvh + 1]]
                        for eo in range(2):
                            rows = slice(64 * eo, 64 * eo + 64)
                            cols = slice(eo * 256, (eo + 1) * 256)
                            qv = Qp[rows, 2 * kvh:2 * kvh + 2, qs]
                            if has_prev:
                                mm(ps[pb_][:, cols], Kd[rows, kvh, blk * 128:(blk + 1) * 128], qv, True, True, [Kd] + qb, [b_ps[pb_]])
                            mm(ps[cb_][:, cols], Kd[rows, kvh, (blk + 1) * 128:(blk + 2) * 128], qv, True, True, [Kd] + qb, [b_ps[cb_]])
                            mm(ps[mb_][:NMETA, cols], Km[rows, kvh, :], qv, True, True, [Km] + qb, [b_ps[mb_]])

                    def emit_rest(i):
                        kvh, blk = steps[i]
                        pb_, cb_, mb_ = PB[i % 2], CB[i % 2], MB[i % 2]
                        ptp, ptc, ptm = PTp[i % 2], PTc[i % 2], PTm[i % 2]
                        ob = OB[i % 2]
                        has_prev = not (ti == 0 and blk == 0)
                        e1, e2 = ("dve", "pool") if i % 2 == 0 else ("pool", "dve")
                        if has_prev:
                            actf(ptp[:], ps[pb_][:], AF.Exp, [b_ps[pb_]], [ptp], scale=scale)
                            v3 = ptp[:].rearrange("p (a q) -> p a q", q=128)
                            tt(e1, v3, v3, mask2[:, 0, :].unsqueeze(1).to_broadcast([128, 4, 128]), ALU.mult, [ptp, mask2], [ptp])
                        actf(ptc[:], ps[cb_][:], AF.Exp, [b_ps[cb_]], [ptc], scale=scale)
                        v3 = ptc[:].rearrange("p (a q) -> p a q", q=128)
                        tt(e2, v3, v3, mask2[:, 1, :].unsqueeze(1).to_broadcast([128, 4, 128]), ALU.mult, [ptc, mask2], [ptc])
                        actf(ptm[:NMETA, :], ps[mb_][:NMETA, :], AF.Exp, [b_ps[mb_]], [ptm], scale=scale)
                        seqm = []
                        for eo in range(2):
                            cols = slice(eo * 256, (eo + 1) * 256)
                            V_ = Vlo if eo == 0 else Vhi
                            Vm_ = Vmlo if eo == 0 else Vmhi
                            o_ = olo if eo == 0 else ohi
                            if has_prev:
                                seqm.append((V_[:, blk, kvh, :], o_[:], ptp[:, cols], [V_.b[blk], o_, ptp]))
                            seqm.append((V_[:, blk + 1, kvh, :], o_[:], ptc[:, cols], [V_.b[blk + 1], o_, ptc]))
                            seqm.append((Vm_[:, kvh, :], o_[:NMETA, :], ptm[:NMETA, cols], [Vm_, o_, ptm]))
                        for ii, (vv, oo, pp_, R) in enumerate(seqm):
                            mm(ps[ob][:, 0:256], vv, pp_, ii == 0, ii == len(seqm) - 1, R, [b_ps[ob]])
                        for ii, (vv, oo, pp_, R) in enumerate(seqm):
                            mm(ps[ob][:, 256:512], oo, pp_, ii == 0, ii == len(seqm) - 1, R, [b_ps[ob]])
                        pending.append((i,))

                    def finalize(i):
                        kvh, blk = steps[i]
                        ob = OB[i % 2]
                        rd = rden[i % 2]
                        qs = slice(blk * 128, (blk + 1) * 128)
                        for pr in range(2):
                            ts("dve", rd[:, pr * 128:(pr + 1) * 128], ps[ob][:, 256 + pr * 128:256 + (pr + 1) * 128],
                               sinkx[:, 2 * kvh + pr:2 * kvh + pr + 1], None, ALU.add, None, [b_ps[ob], sinkx], [rd])
                        actf(rd[:, 0:256], rd[:, 0:256], AF.Ln, [rd], [rd])
                        actf(rd[:, 0:256], rd[:, 0:256], AF.Exp, [rd], [rd], scale=-1.0)
                        tt("dve", oT[:, 2 * kvh:2 * kvh + 2, qs], ps[ob][:, 0:256].rearrange("p (a q) -> p a q", q=128),
                           rd[:, 0:256].rearrange("p (a q) -> p a q", q=128), ALU.mult, [b_ps[ob], rd],
                           [oT.b[2 * kvh], oT.b[2 * kvh + 1]])

                    pending = []
                    emit_scores(0)
                    for i in range(len(steps)):
                        if i + 1 < len(steps):
                            emit_scores(i + 1)
                        fin = list(pending)
                        del pending[:]
                        emit_rest(i)
                        for f_ in fin:
                            finalize(*f_)
                    for f_ in pending:
                        finalize(*f_)
                    for k in range(KC):
                        po = k % 2
                        for j in range(8):
                            mm(ps[po][:, :n], Wo[:, j, k * 128:(k + 1) * 128], oT[:, j, :n], j == 0, j == 7,
                               [Wo, oT.b[j]], [b_ps[po]])
                        tt("dve", c.h[:, k, :n], c.h[:, k, :n], ps[po][:, :n], ALU.add, [c.h.b[k], b_ps[po]], [c.h.b[k]])
                    store_h(c, s, p0, n)

        if UPTO >= 7:
            phase_begin()
            c = ffn_alloc()
            load_ffn_weights(c, 1, 1)
            ffn_phase_pipelined(c, 3, "out")

        dbg = {}
        if 2 <= UPTO < 7:
            P.barrier()
            for nm, src, shp, dt in (("d_hA", hA, [D, LT], F32), ("d_yg", ygs, [512, LT], BF16), ("d_uT", uTs, [512, LT], BF16),
                                     ("d_KT", KTs, [256, LT], BF16), ("d_V", Vs, [LT, 256], BF16)):
                dd = nc.dram_tensor(nm, shp, dt, kind="ExternalOutput").ap()
                bo = P.buf(nm)
                dma("sp", dd, src, [], [bo], bo)
                b_out.append(bo)
        P.finalize(es, b_out + dbg_bufs)
        print("prog stats", P.stats, "arena", st["off"], flush=True)
    return nc


def _rope_tables(L):
    half = HD // 2
    freqs = (np.float32(10000.0) ** (-np.arange(0, half, dtype=np.float32) * np.float32(2.0) / np.float32(HD))).astype(np.float32)
    ang = np.arange(L, dtype=np.float32)[None, :] * freqs[:, None]
    cos = np.cos(ang).astype(np.float32)
    sin = np.sin(ang).astype(np.float32)
    return np.ascontiguousarray(np.stack([np.tile(cos, (4, 1)), np.tile(sin, (4, 1))], 0))


def _rot_matrix():
    R = np.zeros((128, 128), np.float32)
    for hb in (0, 64):
        for dp in range(32):
            R[hb + dp + 32, hb + dp] = -1.0
            R[hb + dp, hb + dp + 32] = 1.0
    return R


def prep_shared(inp, L):
    f = lambda a: np.ascontiguousarray(np.asarray(a, dtype=np.float32))
    m = {}
    m["metaT"] = f(np.asarray(inp["meta_tokens"]).T)
    for l in range(2):
        m[f"wgu{l}0"] = f(inp["ffn1_w_gate_up"][l])
        m[f"wdn{l}0"] = f(inp["ffn1_w_down"][l])
        m[f"wgu{l}1"] = f(inp["ffn2_w_gate_up"][l])
        m[f"wdn{l}1"] = f(inp["ffn2_w_down"][l])
    g = np.stack([inp["ffn1_norm"][0], inp["ffn2_norm"][0], inp["ffn1_norm"][1], inp["ffn2_norm"][1],
                  inp["mix_norm"][0], inp["mix_norm"][1], inp["kv_norm"]], 0)
    m["gains"] = f(np.asarray(g).reshape(7, KC, 128).transpose(2, 0, 1))
    m["w_in"] = f(inp["ssm_w_in"][0])
    m["w_out"] = f(inp["ssm_w_out"][0])
    m["w_kv"] = f(inp["w_kv"])
    m["w_q"] = f(inp["attn_w_q"][0])
    m["w_o"] = f(inp["attn_w_o"][0])
    lr = np.asarray(inp["ssm_lambda_re"][0]); li = np.asarray(inp["ssm_lambda_im"][0])
    ls = np.broadcast_to(np.asarray(inp["ssm_log_step"][0])[:, None], (NG, NP))
    toS = lambda a: np.asarray(a).reshape(16, 2, NP).transpose(1, 2, 0).reshape(128, 16)
    m["lamS"] = f(np.stack([toS(lr), toS(li), toS(ls)], 1))
    br = np.asarray(inp["ssm_b_re"][0]); bi = np.asarray(inp["ssm_b_im"][0])
    cr = np.asarray(inp["ssm_c_re"][0]); ci = np.asarray(inp["ssm_c_im"][0])
    bS = lambda a: a.reshape(16, 2, NP, 16).transpose(1, 2, 0, 3).reshape(128, 256)
    cS = lambda a: a.reshape(16, 2, 16, NP).transpose(1, 3, 0, 2).reshape(128, 256)
    m["bcS"] = f(np.stack([bS(br), bS(bi), cS(cr), cS(ci)], 1))
    q = np.arange(128)
    gidx = (8 * np.arange(4)[None, :] + (q // 16)[:, None])
    toC = lambda a: np.asarray(a)[gidx].reshape(128, 256)
    m["lamC"] = f(np.stack([toC(lr), toC(li), toC(ls)], 1))
    bCf = lambda a: a[gidx, :, (q % 16)[:, None]].reshape(128, 256)
    m["bC"] = f(np.stack([bCf(br), bCf(bi)], 1))
    m["dC"] = f(np.asarray(inp["ssm_d"][0]).reshape(4, 128).T)
    m["qkg"] = f(np.stack([np.tile(np.asarray(inp["q_norm"][0]), 2), np.tile(np.asarray(inp["k_norm"]), 2)], 1))
    sk = np.asarray(inp["attn_sinks"][0])
    m["sinkL"] = f(sk.reshape(8, 2)[:, (q // 64)].T)
    m["ropeT"] = _rope_tables(L)
    m["rmat"] = _rot_matrix()
    return m


_NC = {}


def kernel(**inputs):
    x = np.asarray(inputs["x"], dtype=np.float32)
    B, S, _ = x.shape
    ncores = 8
    nseq = B // ncores
    key = (S, nseq)
    if key not in _NC:
        _NC[key] = build(Cfg(seq=S, nseq=nseq))
    nc = _NC[key]
    shared = prep_shared(inputs, S + NMETA)
    in_maps = []
    for c in range(ncores):
        m = dict(shared)
        m["xT"] = np.ascontiguousarray(x[c * nseq:(c + 1) * nseq].reshape(nseq * S, D).T)
        in_maps.append(m)
    res = run_bass_kernel_spmd(nc, in_maps, core_ids=list(range(ncores)))
    out = np.empty((B, S, D), np.float32)
    for c in range(ncores):
        out[c * nseq:(c + 1) * nseq] = res.results[c]["outT"].T.reshape(nseq, S, D)
    return out
```

```python
import numpy as np
from contextlib import ExitStack
import concourse.bass as bass
import concourse.mybir as mybir
from concourse.bass_utils import run_bass_kernel_spmd

F32 = mybir.dt.float32
BF16 = mybir.dt.bfloat16
AF = mybir.ActivationFunctionType
ALU = mybir.AluOpType

D = 1024
DFF = 2816
NF = DFF // 128
KC = D // 128
NMETA = 16
EPS = 1e-6


class Buf:
    __slots__ = ("name", "writers", "readers", "slot")

    def __init__(self, name):
        self.name = name
        self.writers = []
        self.readers = []
        self.slot = None


class Op:
    __slots__ = ("eng", "emit", "deps", "dma", "sig", "sem", "val", "vc", "slotbuf", "idx")


class Prog:
    ENGS = ("pe", "act", "dve", "pool", "sp")

    def __init__(self, nc):
        self.nc = nc
        self.ops = []
        self.nbuf = 0
        self.fence = None
        self.fence_idx = 0
        self.capture = None
        self.captured = []

    def buf(self, name=None):
        self.nbuf += 1
        return Buf(name or f"b{self.nbuf}")

    def bufs(self, n, name="b"):
        return [self.buf(f"{name}{i}") for i in range(n)]

    def op(self, eng, emit, reads=(), writes=(), dma=False, slot=None):
        if self.capture:
            self.captured.append((eng, emit, list(reads), list(writes), dma, slot))
            return None
        o = Op()
        o.eng = eng
        o.emit = emit
        o.dma = dma
        o.sig = False
        o.sem = None
        o.val = 0
        o.vc = None
        o.slotbuf = slot
        o.idx = len(self.ops)
        deps = []
        if self.fence is not None:
            deps.append(self.fence)
        for b in reads:
            for w in b.writers:
                deps.append(w)
        for b in writes:
            for w in b.writers:
                deps.append(w)
            for r in b.readers:
                deps.append(r)
        fd = []
        for d in deps:
            if d is o:
                continue
            if (not d.dma) and (not dma) and d.eng == eng:
                if eng == "pe":
                    continue
            fd.append(d)
        o.deps = fd
        for d in fd:
            d.sig = True
        for b in reads:
            b.readers.append(o)
        for b in writes:
            b.writers = [o]
            b.readers = []
        self.ops.append(o)
        return o

    def splice(self, k=None):
        n = len(self.captured) if k is None else min(k, len(self.captured))
        todo, self.captured = self.captured[:n], self.captured[n:]
        cap, self.capture = self.capture, None
        for a in todo:
            self.op(*a)
        self.capture = cap

    def barrier(self):
        last = {}
        deps = []
        for o in self.ops[self.fence_idx:]:
            if o.dma:
                deps.append(o)
            elif o.eng != "sp":
                last[o.eng] = o
        deps += list(last.values())
        if self.fence is not None and not deps:
            return
        B = Op()
        B.eng = "sp"
        B.emit = lambda e: e.nop()
        B.dma = False
        B.sig = True
        B.sem = None
        B.val = 0
        B.vc = None
        B.slotbuf = None
        B.idx = len(self.ops)
        B.deps = deps
        for d in deps:
            d.sig = True
        self.ops.append(B)
        self.fence = B
        self.fence_idx = len(self.ops)

    def dma(self, eng, out, in_, reads, writes, slot):
        return self.op(eng, lambda e: e.dma_start(out=out, in_=in_), reads, writes, dma=True, slot=slot)

    def finalize(self, es, final_reads):
        nc = self.nc
        self.op("sp", None, reads=final_reads)
        esem = {e: es.enter_context(nc.semaphore(f"s_{e}")) for e in self.ENGS}
        ecount = {e: 0 for e in self.ENGS}
        slotsem = {}
        slotcount = {}
        waited = {e: {} for e in self.ENGS}
        streams = {e: [] for e in self.ENGS}
        nwait = 0
        for o in self.ops:
            E = o.eng
            wl = []
            wd = waited[E]
            need = {}
            for d in o.deps:
                k = id(d.sem)
                if wd.get(k, (None, 0))[1] >= d.val:
                    continue
                if k not in need or need[k][1] < d.val:
                    need[k] = (d.sem, d.val, d)
            for k, (sem, val, d) in need.items():
                if wd.get(k, (None, 0))[1] >= val:
                    continue
                wl.append((sem, val))
                for k2, (s2, v2) in d.vc.items():
                    if wd.get(k2, (None, 0))[1] < v2:
                        wd[k2] = (s2, v2)
                wd[k] = (sem, val)
            nwait += len(wl)
            if o.sig:
                if o.dma:
                    sb = o.slotbuf
                    if id(sb) not in slotsem:
                        slotsem[id(sb)] = es.enter_context(nc.semaphore(f"d_{len(slotsem)}"))
                        slotcount[id(sb)] = 0
                    slotcount[id(sb)] += 16
                    o.sem = slotsem[id(sb)]
                    o.val = slotcount[id(sb)]
                    o.vc = dict(wd)
                    o.vc[id(o.sem)] = (o.sem, o.val)
                else:
                    ecount[E] += 1
                    o.sem = esem[E]
                    o.val = ecount[E]
                    o.vc = dict(wd)
                    o.vc[id(o.sem)] = (o.sem, o.val)
            streams[E].append((wl, o))
        self.stats = dict(nops=len(self.ops), nwait=nwait, nsem=len(slotsem) + 5,
                          per_eng={e: len(streams[e]) for e in self.ENGS})
        block = es.enter_context(nc.Block())

        def run(engobj, lst):
            for wl, o in lst:
                for sem, val in wl:
                    engobj.wait_ge(sem, val)
                if o.emit is None:
                    continue
                ins = o.emit(engobj)
                if o.sig:
                    ins.then_inc(o.sem, 16 if o.dma else 1)

        @block.sync
        def _(e):
            run(e, streams["sp"])

        @block.tensor
        def _(e):
            run(e, streams["pe"])

        @block.scalar
        def _(e):
            run(e, streams["act"])

        @block.vector
        def _(e):
            run(e, streams["dve"])

        @block.gpsimd
        def _(e):
            run(e, streams["pool"])


import math

HD = 64
NG = 32
NP = 64
TB = 16
TWO_PI = 2.0 * math.pi
GELU_C = 2.0 * math.sqrt(2.0 / math.pi)


class Cfg:
    def __init__(self, seq=4096, nseq=2, upto=99):
        self.seq = seq
        self.nseq = nseq
        self.L = seq + NMETA
        self.upto = upto


class T:
    __slots__ = ("ap", "b")

    def __init__(self, ap, b):
        self.ap = ap
        self.b = b

    def __getitem__(self, k):
        return self.ap[k]


def build(cfg):
    nc = bass.Bass("TRN2", target_bir_lowering=False)
    SEQ, NSEQ, L = cfg.seq, cfg.nseq, cfg.L
    NTOK = NSEQ * SEQ
    LT = NSEQ * L
    NB = L // TB
    NTILE = SEQ // 512
    UPTO = cfg.upto

    def din(name, shape, dt=F32):
        return nc.dram_tensor(name, list(shape), dt, kind="ExternalInput").ap()

    def dscr(name, shape, dt):
        return nc.dram_tensor(name, list(shape), dt, kind="Internal").ap()

    xT = din("xT", [D, NTOK])
    metaT = din("metaT", [D, NMETA])
    w_gu = [[din(f"wgu{l}{i}", [D, 2 * DFF]) for i in range(2)] for l in range(2)]
    w_dn = [[din(f"wdn{l}{i}", [DFF, D]) for i in range(2)] for l in range(2)]
    gains = din("gains", [128, 7, KC])
    w_in = din("w_in", [D, 512])
    w_out = din("w_out", [512, 2 * D])
    w_kv = din("w_kv", [D, 512])
    w_q = din("w_q", [D, D])
    w_o = din("w_o", [D, D])
    lamS_d = din("lamS", [128, 3, 16])
    bcS_d = din("bcS", [128, 4, 256])
    lamC_d = din("lamC", [128, 3, 256])
    bC_d = din("bC", [128, 2, 256])
    dC_d = din("dC", [128, 4])
    qkg_d = din("qkg", [128, 2])
    sink_d = din("sinkL", [128, 8])
    rope_d = din("ropeT", [2, 128, L])
    rmat_d = din("rmat", [128, 128])
    outT = nc.dram_tensor("outT", [D, NTOK], F32, kind="ExternalOutput").ap()
    hA = dscr("hA", [D, LT], F32)
    uTs = dscr("uTs", [512, LT], BF16)
    ygs = dscr("ygs", [512, LT], BF16)
    KTs = dscr("KTs", [256, LT], BF16)
    Vs = dscr("Vs", [LT, 256], BF16)

    es = ExitStack()
    with es:
        P = Prog(nc)
        ARB = 210944
        AR = es.enter_context(nc.sbuf_tensor("AR", [128, ARB // 2], BF16))
        st = {"off": 0}

        def alloc(shape, dt, name=None, nb=1):
            esz = 4 if dt == F32 else (4 if dt == mybir.dt.int32 else 2)
            n = 1
            for x in shape[1:]:
                n *= x
            nbytes = (n * esz + 31) // 32 * 32
            o = st["off"]
            assert o + nbytes <= ARB, f"arena overflow {name} {o + nbytes}"
            st["off"] = o + nbytes
            v = AR[0:shape[0], o // 2:(o + n * esz) // 2]
            if dt != BF16:
                v = v.bitcast(dt)
            if len(shape) > 2:
                names = " ".join(f"a{i}" for i in range(len(shape) - 1))
                kw = {f"a{i}": shape[i + 1] for i in range(len(shape) - 1)}
                v = v.rearrange(f"p ({names}) -> p {names}", **kw)
            b = P.buf(name) if nb == 1 else P.bufs(nb, name or "t")
            return T(v, b)

        psall = es.enter_context(nc.psum_tensor("psall", [128, 4096], F32))
        ps = [psall[:, i * 512:(i + 1) * 512] for i in range(8)]
        b_ps = P.bufs(8, "ps")

        def bl(x):
            out = []
            for t in x:
                if isinstance(t, Buf):
                    out.append(t)
                elif isinstance(t.b, list):
                    out.extend(t.b)
                else:
                    out.append(t.b)
            return out

        def tt(eng, out, a, b, op, R, W):
            P.op(eng, lambda e: e.tensor_tensor(out=out, in0=a, in1=b, op=op), bl(R), bl(W))

        def ts(eng, out, a, s1, s2, op0, op1, R, W):
            if s2 is None:
                s2, op1 = 0.0, ALU.add
            P.op(eng, lambda e: e.tensor_scalar(out=out, in0=a, scalar1=s1, scalar2=s2, op0=op0, op1=op1), bl(R), bl(W))

        def stt(eng, out, a, sc, b, op0, op1, R, W):
            P.op(eng, lambda e: e.scalar_tensor_tensor(out=out, in0=a, scalar=sc, in1=b, op0=op0, op1=op1), bl(R), bl(W))

        def actf(out, a, func, R, W, scale=1.0, bias=0.0):
            P.op("act", lambda e: e.activation(out=out, in_=a, func=func, scale=scale, bias=bias), bl(R), bl(W))

        def cp(eng, out, a, R, W):
            if eng == "act":
                P.op("act", lambda e: e.activation(out=out, in_=a, func=AF.Copy), bl(R), bl(W))
            else:
                P.op(eng, lambda e: e.tensor_copy(out=out, in_=a), bl(R), bl(W))

        def mset(eng, out, val, W):
            P.op(eng, lambda e: e.memset(out, val), [], bl(W))

        def mm(out, lhsT, rhs, start, stop, R, W):
            P.op("pe", lambda e: e.matmul(out, lhsT=lhsT, rhs=rhs, start=start, stop=stop), bl(R), bl(W))

        def dma(eng, out, in_, R, W, slot):
            sl = slot.b if isinstance(slot, T) else slot
            if isinstance(sl, list):
                sl = sl[0]
            P.dma(eng, out, in_, bl(R), bl(W), sl)

        DBG = getattr(cfg, "dbg", False)
        dbg_bufs = []

        def dump(name, t, shape, dt):
            if not DBG:
                return
            dd = nc.dram_tensor("z_" + name, list(shape), dt, kind="ExternalOutput").ap()
            bo = P.buf("z" + name)
            src = t.ap if isinstance(t, T) else t
            P.dma("sp", dd, src, bl([t]) if isinstance(t, T) else [], [bo], bo)
            dbg_bufs.append(bo)

        ones = alloc([128, 128], BF16, "ones")
        gn = alloc([128, 7, KC], F32, "gn")
        rmat = alloc([128, 128], BF16, "rmat")
        qkg = alloc([128, 2], F32, "qkg")
        sinkx = alloc([128, 8], F32, "sinkx")
        dC = alloc([128, 4], F32, "dC")
        mset("pool", ones[:], 1.0, [ones])
        ones2 = alloc([128, 128], BF16, "ones2")
        mset("pool", ones2[:], 0.0, [ones2])
        mset("pool", ones2[0:64, 0:64], 1.0, [ones2])
        mset("pool", ones2[64:128, 64:128], 1.0, [ones2])
        dma("sp", gn[:], gains, [], [gn], gn)
        dma("sp", qkg[:], qkg_d, [], [qkg], qkg)
        dma("sp", sinkx[:], sink_d, [], [sinkx], sinkx)
        dma("sp", dC[:], dC_d, [], [dC], dC)
        dma("pool", rmat[:], rmat_d, [], [rmat], rmat)
        actf(sinkx[:], sinkx[:], AF.Exp, [sinkx], [sinkx])
        PERSIST = st["off"]

        xTv = xT.rearrange("(k p) t -> p k t", p=128)
        metaTv = metaT.rearrange("(k p) t -> p k t", p=128)
        hAv = hA.rearrange("(k p) t -> p k t", p=128)
        outTv = outT.rearrange("(k p) t -> p k t", p=128)
        uTv = uTs.rearrange("(c p) t -> p c t", p=128)
        ygv = ygs.rearrange("(c p) t -> p c t", p=128)
        tiles = [(0, NMETA)] + [(NMETA + 512 * i, 512) for i in range(NTILE)]
        qtiles = tiles[1:]
        b_hA = {(s, p0): P.buf(f"hA{s}_{p0}") for s in range(NSEQ) for (p0, n) in tiles}
        b_uT = {(s, p0): P.buf(f"uT{s}_{p0}") for s in range(NSEQ) for (p0, n) in tiles}
        b_yg = {s: P.buf(f"yg{s}") for s in range(NSEQ)}
        b_kv = {(s, p0): P.buf(f"kv{s}_{p0}") for s in range(NSEQ) for (p0, n) in tiles}
        b_out = []

        class FFNCtx:
            pass

        def ffn_alloc(with_weights=True, with_sg=True):
            c = FFNCtx()
            c.h = alloc([128, KC, 512], F32, "h", nb=KC)
            c.hn = alloc([128, KC, 512], BF16, "hn", nb=KC)
            c.sq = [alloc([128, 512], BF16, f"sq{i}") for i in range(2)]
            if with_sg:
                c.sg = [alloc([128, 512], F32, f"sg{i}") for i in range(2)]
            c.rstd = alloc([128, 512], F32, "rstd")
            if with_weights:
                c.Wgu = alloc([128, KC, 2 * DFF], BF16, "Wgu")
                c.Wdn = alloc([128, NF, D], BF16, "Wdn")
                c.act = alloc([128, NF, 512], BF16, "act", nb=NF)
            return c

        def load_ffn_weights(c, l, i):
            dma("pool", c.Wgu[:], w_gu[l][i].rearrange("(k p) f -> p k f", p=128), [], [c.Wgu], c.Wgu)
            dma("pool", c.Wdn[:], w_dn[l][i].rearrange("(f p) d -> p f d", p=128), [], [c.Wdn], c.Wdn)

        def rmsnorm(c, n, gi, h=None):
            h = c.h if h is None else h
            hn, sq, rstd = c.hn, c.sq, c.rstd
            for k in range(KC):
                s = k % 2
                actf(sq[s][:, :n], h[:, k, :n], AF.Square, [h.b[k]], [sq[s]])
                mm(ps[6][:, :n], ones[:], sq[s][:, :n], k == 0, k == KC - 1, [ones, sq[s]], [b_ps[6]])
            actf(rstd[:, :n], ps[6][:, :n], AF.Ln, [b_ps[6]], [rstd], scale=1.0 / D, bias=EPS)
            actf(rstd[:, :n], rstd[:, :n], AF.Exp, [rstd], [rstd], scale=-0.5)
            for k in range(KC):
                stt("dve", hn[:, k, :n], h[:, k, :n], gn[:, gi, k:k + 1], rstd[:, :n], ALU.mult, ALU.mult,
                    [h.b[k], gn, rstd], [hn.b[k]])

        def ffn_gateup(c, n):
            hn, sg, act, Wgu = c.hn, c.sg, c.act, c.Wgu
            for f in range(NF):
                pa, pb = 2 * (f % 2), 2 * (f % 2) + 1
                for k in range(KC):
                    mm(ps[pa][:, :n], Wgu[:, k, f * 128:(f + 1) * 128], hn[:, k, :n], k == 0, k == KC - 1,
                       [Wgu, hn.b[k]], [b_ps[pa]])
                for k in range(KC):
                    mm(ps[pb][:, :n], Wgu[:, k, DFF + f * 128:DFF + (f + 1) * 128], hn[:, k, :n], k == 0, k == KC - 1,
                       [Wgu, hn.b[k]], [b_ps[pb]])
                s = f % 2
                actf(sg[s][:, :n], ps[pa][:, :n], AF.Silu, [b_ps[pa]], [sg[s]])
                tt("dve", act[:, f, :n], sg[s][:, :n], ps[pb][:, :n], ALU.mult, [sg[s], b_ps[pb]], [act.b[f]])

        def ffn_down(c, n, h=None):
            h = c.h if h is None else h
            act, Wdn = c.act, c.Wdn
            for k in range(KC):
                pd = 4 + (k % 2)
                for f in range(NF):
                    mm(ps[pd][:, :n], Wdn[:, f, k * 128:(k + 1) * 128], act[:, f, :n], f == 0, f == NF - 1,
                       [Wdn, act.b[f]], [b_ps[pd]])
                stt("dve", h[:, k, :n], ps[pd][:, :n], 0.5, h[:, k, :n], ALU.mult, ALU.add,
                    [b_ps[pd], h.b[k]], [h.b[k]])

        def ffn(c, n, gi):
            rmsnorm(c, n, gi)
            ffn_gateup(c, n)
            ffn_down(c, n)

        def ffn_phase_pipelined(c, gi, mode):
            hb = [c.h, alloc([128, KC, 512], F32, "h2", nb=KC)]
            tls = tiles if mode in ("in", "mid") else qtiles
            tl = [(s, p0, n) for s in range(NSEQ) for (p0, n) in tls]

            def ld(i):
                s, p0, n = tl[i]
                h = hb[i % 2]
                if mode == "in":
                    if p0 == 0:
                        dma("sp", h[:, :, :n], metaTv, [], [h], h.b[0])
                    else:
                        c0 = s * SEQ + p0 - NMETA
                        dma("sp", h[:, :, :n], xTv[:, :, c0:c0 + n], [], [h], h.b[0])
                else:
                    dma("sp", h[:, :, :n], hAv[:, :, s * L + p0:s * L + p0 + n], [b_hA[(s, p0)]], [h], h.b[0])
                rmsnorm(c, n, gi, h=h)

            ld(0)
            for i, (s, p0, n) in enumerate(tl):
                h = hb[i % 2]
                ffn_gateup(c, n)
                if i + 1 < len(tl):
                    ld(i + 1)
                ffn_down(c, n, h=h)
                if mode == "out":
                    c0 = s * SEQ + p0 - NMETA
                    bo = P.buf("o")
                    dma("pool", outTv[:, :, c0:c0 + n], h[:, :, :n], [h], [bo], h.b[1])
                    b_out.append(bo)
                else:
                    dma("pool", hAv[:, :, s * L + p0:s * L + p0 + n], h[:, :, :n], [h], [b_hA[(s, p0)]], h.b[1])

        def load_h(c, s, p0, n):
            dma("sp", c.h[:, :, :n], hAv[:, :, s * L + p0:s * L + p0 + n], [b_hA[(s, p0)]], [c.h], c.h.b[0])

        def store_h(c, s, p0, n):
            dma("pool", hAv[:, :, s * L + p0:s * L + p0 + n], c.h[:, :, :n], [c.h], [b_hA[(s, p0)]], c.h.b[1])

        def store_out(c, s, p0, n):
            c0 = s * SEQ + p0 - NMETA
            bo = P.buf("o")
            dma("sp", outTv[:, :, c0:c0 + n], c.h[:, :, :n], [c.h], [bo], c.h.b[1])
            b_out.append(bo)

        def phase_begin():
            P.barrier()
            st["off"] = PERSIST

        phase_begin()
        c = ffn_alloc()
        load_ffn_weights(c, 0, 0)
        if UPTO == 1:
            for s in range(NSEQ):
                for (p0, n) in tiles:
                    if p0 == 0:
                        dma("sp", c.h[:, :, :n], metaTv, [], [c.h], c.h.b[0])
                    else:
                        c0 = s * SEQ + p0 - NMETA
                        dma("sp", c.h[:, :, :n], xTv[:, :, c0:c0 + n], [], [c.h], c.h.b[0])
                    ffn(c, n, 0)
                    if p0 > 0:
                        store_out(c, s, p0, n)
        else:
            ffn_phase_pipelined(c, 0, "in")
            phase_begin()
            P.capture = True
            def lam_setup(src_d, F, tag):
                lam = alloc([128, 3, F], F32, "lam" + tag)
                dma("sp", lam[:], src_d, [], [lam], lam)
                r = {}
                for nm in ("step", "a", "th", "ar", "ai", "cr", "ci", "t1", "t2", "t3", "kk"):
                    r[nm] = alloc([128, F], F32, nm + tag)
                r["Pr"] = alloc([128, 17, F], F32, "Pr" + tag)
                r["Pi"] = alloc([128, 17, F], F32, "Pi" + tag)
                lr, li, ls = lam[:, 0, :], lam[:, 1, :], lam[:, 2, :]
                actf(r["step"][:], ls, AF.Exp, [lam], [r["step"]])
                tt("dve", r["a"][:], lr, r["step"][:], ALU.mult, [lam, r["step"]], [r["a"]])
                tt("dve", r["th"][:], li, r["step"][:], ALU.mult, [lam, r["step"]], [r["th"]])
                actf(r["t3"][:], r["a"][:], AF.Exp, [r["a"]], [r["t3"]])
                for (dst, shift) in (("ai", math.pi), ("ar", 1.5 * math.pi)):
                    ts("dve", r["t1"][:], r["th"][:], shift, None, ALU.add, None, [r["th"]], [r["t1"]])
                    mset("dve", r["kk"][:], 0.0, [r["kk"]])
                    for m in range(1, 17):
                        stt("dve", r["kk"][:], r["t1"][:], TWO_PI * m, r["kk"][:], ALU.is_ge, ALU.add,
                            [r["t1"], r["kk"]], [r["kk"]])
                    stt("dve", r["t2"][:], r["kk"][:], -TWO_PI, r["t1"][:], ALU.mult, ALU.add,
                        [r["kk"], r["t1"]], [r["t2"]])
                    ts("dve", r["t2"][:], r["t2"][:], -math.pi, None, ALU.add, None, [r["t2"]], [r["t2"]])
                    actf(r[dst][:], r["t2"][:], AF.Sin, [r["t2"]], [r[dst]])
                    tt("dve", r[dst][:], r[dst][:], r["t3"][:], ALU.mult, [r[dst], r["t3"]], [r[dst]])
                tt("dve", r["t1"][:], lr, lr, ALU.mult, [lam], [r["t1"]])
                tt("dve", r["t2"][:], li, li, ALU.mult, [lam], [r["t2"]])
                tt("dve", r["t1"][:], r["t1"][:], r["t2"][:], ALU.add, [r["t1"], r["t2"]], [r["t1"]])
                P.op("dve", lambda e: e.reciprocal(out=r["t1"][:], in_=r["t1"][:]), bl([r["t1"]]), bl([r["t1"]]))
                ts("dve", r["t3"][:], r["ar"][:], -1.0, None, ALU.add, None, [r["ar"]], [r["t3"]])
                tt("dve", r["t2"][:], r["t3"][:], lr, ALU.mult, [r["t3"], lam], [r["t2"]])
                tt("dve", r["kk"][:], r["ai"][:], li, ALU.mult, [r["ai"], lam], [r["kk"]])
                tt("dve", r["t2"][:], r["t2"][:], r["kk"][:], ALU.add, [r["t2"], r["kk"]], [r["t2"]])
                tt("dve", r["cr"][:], r["t2"][:], r["t1"][:], ALU.mult, [r["t2"], r["t1"]], [r["cr"]])
                tt("dve", r["t2"][:], r["ai"][:], lr, ALU.mult, [r["ai"], lam], [r["t2"]])
                tt("dve", r["kk"][:], r["t3"][:], li, ALU.mult, [r["t3"], lam], [r["kk"]])
                tt("dve", r["t2"][:], r["t2"][:], r["kk"][:], ALU.subtract, [r["t2"], r["kk"]], [r["t2"]])
                tt("dve", r["ci"][:], r["t2"][:], r["t1"][:], ALU.mult, [r["t2"], r["t1"]], [r["ci"]])
                Pr, Pi = r["Pr"], r["Pi"]
                mset("dve", Pr[:, 0, :], 1.0, [Pr])
                mset("dve", Pi[:, 0, :], 0.0, [Pi])
                for k in range(1, 17):
                    cmul("dve", Pr[:, k, :], Pi[:, k, :], Pr[:, k - 1, :], Pi[:, k - 1, :], r["ar"][:], r["ai"][:],
                         r["t1"], r["t2"], [Pr, Pi, r["ar"], r["ai"]], [Pr, Pi])
                return r

            def cmul(eng, o_re, o_im, a_re, a_im, b_re, b_im, t1, t2, R, W, neg_im=False):
                sh = tuple(a_re.shape)
                v1, v2 = t1.ap, t2.ap
                if tuple(v1.shape) != sh:
                    n = 1
                    for x in sh[1:]:
                        n *= x
                    v1 = t1.ap[0:sh[0], 0:n]
                    v2 = t2.ap[0:sh[0], 0:n]
                    if len(sh) == 3:
                        v1 = v1.rearrange("p (a b) -> p a b", b=sh[2])
                        v2 = v2.rearrange("p (a b) -> p a b", b=sh[2])
                    elif len(sh) == 4:
                        v1 = v1.rearrange("p (a b c) -> p a b c", b=sh[2], c=sh[3])
                        v2 = v2.rearrange("p (a b c) -> p a b c", b=sh[2], c=sh[3])
                tt(eng, v1, a_re, b_re, ALU.mult, R, [t1])
                tt(eng, v2, a_im, b_im, ALU.mult, R, [t2])
                tt(eng, o_re, v1, v2, ALU.subtract, [t1, t2], W)
                tt(eng, v1, a_re, b_im, ALU.mult, R, [t1])
                tt(eng, v2, a_im, b_re, ALU.mult, R, [t2])
                if neg_im:
                    stt(eng, o_im, v1, -1.0, v2, ALU.mult, ALU.subtract, [t1, t2], W)
                else:
                    tt(eng, o_im, v1, v2, ALU.add, [t1, t2], W)

            VB = alloc([128, 2, 16, 16], BF16, "VB")
            VR = alloc([128, 17, 2, 16, 16], BF16, "VR")
            VI = alloc([128, TB, 2, 256], BF16, "VI")
            r16 = alloc([128, 16], F32, "r16")
            Tr = alloc([128, 16, NB], F32, "Tr")
            Ti = alloc([128, 16, NB], F32, "Ti")
            maskS = alloc([128, 4, 4, 2], BF16, "maskS")
            maskC = alloc([128, 4, 2], BF16, "maskC")
            SSM_KEEP = st["off"]
            rS = lam_setup(lamS_d, 16, "S")
            rC = lam_setup(lamC_d, 256, "C")
            tmpA = alloc([128, 512], F32, "tmpA")
            tmpB = alloc([128, 512], F32, "tmpB")
            bcS = alloc([128, 4, 16, 16], F32, "bcS")
            dma("sp", bcS[:], bcS_d.rearrange("p a (g c) -> p a g c", c=16), [], [bcS], bcS)
            bc16 = lambda ap: ap.unsqueeze(2).to_broadcast([128, 16, 16])
            VBf = alloc([128, 2, 16, 16], F32, "VBf")
            cmul("dve", VBf[:, 0], VBf[:, 1], bc16(rS["cr"][:]), bc16(rS["ci"][:]), bcS[:, 0], bcS[:, 1], tmpA, tmpB,
                 [rS["cr"], rS["ci"], bcS], [VBf])
            cp("dve", VB[:], VBf[:], [VBf], [VB])
            for k in range(17):
                cmul("dve", VR[:, k, 0], VR[:, k, 1], bcS[:, 2], bcS[:, 3], bc16(rS["Pr"][:, k, :]), bc16(rS["Pi"][:, k, :]),
                     tmpA, tmpB, [bcS, rS["Pr"], rS["Pi"]], [VR], neg_im=True)
            bC = alloc([128, 2, 256], F32, "bC")
            dma("sp", bC[:], bC_d, [], [bC], bC)
            VBC = alloc([128, 2, 256], F32, "VBC")
            cmul("dve", VBC[:, 0], VBC[:, 1], rC["cr"][:], rC["ci"][:], bC[:, 0], bC[:, 1], tmpA, tmpB,
                 [rC["cr"], rC["ci"], bC], [VBC])
            for j in range(TB):
                k = TB - 1 - j
                cmul("pool", VI[:, j, 0], VI[:, j, 1], rC["Pr"][:, k, :], rC["Pi"][:, k, :], VBC[:, 0], VBC[:, 1],
                     tmpA, tmpB, [rC["Pr"], rC["Pi"], VBC], [VI])
            actf(r16[:], rS["a"][:], AF.Exp, [rS["a"]], [r16], scale=float(TB))
            ir16 = alloc([128, 16], F32, "ir16")
            P.op("dve", lambda e: e.reciprocal(out=ir16[:], in_=r16[:]), bl([r16]), bl([ir16]))
            cur = alloc([128, 2, 16], F32, "cur")
            cur2 = alloc([128, 2, 16], F32, "cur2")
            tt("dve", cur[:, 0, :], rS["Pr"][:, 16, :], ir16[:], ALU.mult, [rS["Pr"], ir16], [cur])
            tt("dve", cur[:, 1, :], rS["Pi"][:, 16, :], ir16[:], ALU.mult, [rS["Pi"], ir16], [cur])
            mset("dve", Tr[:, :, 0:1], 1.0, [Tr])
            mset("dve", Ti[:, :, 0:1], 0.0, [Ti])
            w = 1
            a_, b_ = cur, cur2
            while w < NB:
                hi = min(2 * w, NB)
                cnt = hi - w
                gstep = max(1, min(16, 512 // cnt))
                for g0 in range(0, 16, gstep):
                    g1 = min(16, g0 + gstep)
                    bcw = lambda ap, cnt=cnt, g0=g0, g1=g1: ap.unsqueeze(2).to_broadcast([128, g1 - g0, cnt])
                    cmul("dve", Tr[:, g0:g1, w:hi], Ti[:, g0:g1, w:hi], Tr[:, g0:g1, 0:cnt], Ti[:, g0:g1, 0:cnt],
                         bcw(a_[:, 0, g0:g1]), bcw(a_[:, 1, g0:g1]), tmpA, tmpB, [Tr, Ti, a_], [Tr, Ti])
                cmul("dve", b_[:, 0, :], b_[:, 1, :], a_[:, 0, :], a_[:, 1, :], a_[:, 0, :], a_[:, 1, :], tmpA, tmpB,
                     [a_], [b_])
                a_, b_ = b_, a_
                w *= 2
            mset("pool", maskS[:], 0.0, [maskS])
            for g in range(4):
                mset("pool", maskS[0:64, g, g, 0:1], 1.0, [maskS])
                mset("pool", maskS[64:128, g, g, 1:2], 1.0, [maskS])
            iot = alloc([128, 4, 2], mybir.dt.int32, "iot")
            iof = alloc([128, 4, 2], F32, "iof")
            iog = alloc([128, 4, 2], F32, "iog")
            P.op("pool", lambda e: e.iota(iot[:], pattern=[[-32, 4], [-16, 2]], base=0, channel_multiplier=1), [], bl([iot]))
            cp("dve", iof[:], iot[:], [iot], [iof])
            ts("dve", iog[:], iof[:], 0.0, None, ALU.is_ge, None, [iof], [iog])
            ts("dve", iof[:], iof[:], 16.0, None, ALU.is_lt, None, [iof], [iof])
            tt("dve", maskC[:], iof[:], iog[:], ALU.mult, [iof, iog], [maskC])

            for nm in ("ar", "ai", "cr", "ci", "th", "a"):
                dump("S_" + nm, rS[nm], [128, 16], F32)
            dump("S_Pr", rS["Pr"], [128, 17, 16], F32)
            dump("S_Pi", rS["Pi"], [128, 17, 16], F32)
            dump("C_Pr", rC["Pr"], [128, 17, 256], F32)
            dump("VB", VB, [128, 2, 16, 16], BF16)
            dump("VR", VR, [128, 17, 2, 16, 16], BF16)
            dump("VI", VI, [128, TB, 2, 256], BF16)
            dump("r16", r16, [128, 16], F32)
            dump("Tr", Tr, [128, 16, NB], F32)
            dump("Ti", Ti, [128, 16, NB], F32)
            dump("maskS", maskS, [128, 4, 4, 2], BF16)
            dump("maskC", maskC, [128, 4, 2], BF16)
            P.capture = False
            n_setup = len(P.captured)
            cs1 = [ffn_alloc(with_weights=False, with_sg=False) for _ in range(2)]
            Win = alloc([128, KC, 512], BF16, "Win")
            ubs = [alloc([128, 4, 512], BF16, f"ub{i}") for i in range(2)]
            dma("pool", Win[:], w_in.rearrange("(k p) f -> p k f", p=128), [], [Win], Win)
            tl1 = [(s, p0, n) for s in range(NSEQ) for (p0, n) in tiles]
            load_h(cs1[0], tl1[0][0], tl1[0][1], tl1[0][2])
            rmsnorm(cs1[0], tl1[0][2], 4)
            for i, (s, p0, n) in enumerate(tl1):
                c = cs1[i % 2]
                ub = ubs[i % 2]
                if i + 1 < len(tl1):
                    load_h(cs1[(i + 1) % 2], tl1[i + 1][0], tl1[i + 1][1], tl1[i + 1][2])
                for ct in range(4):
                    for k in range(KC):
                        mm(ps[ct][:, :n], Win[:, k, ct * 128:(ct + 1) * 128], c.hn[:, k, :n], k == 0, k == KC - 1,
                           [Win, c.hn.b[k]], [b_ps[ct]])
                if i + 1 < len(tl1):
                    rmsnorm(cs1[(i + 1) % 2], tl1[i + 1][2], 4)
                for ct in range(4):
                    cp("act" if ct % 2 == 0 else "pool" if False else "dve", ub[:, ct, :n], ps[ct][:, :n], [b_ps[ct]], [ub])
                dma("pool", uTv[:, :, s * L + p0:s * L + p0 + n], ub[:, :, :n], [ub], [b_uT[(s, p0)]], ub)
                P.splice((n_setup + len(tl1) - 1) // len(tl1))
            P.splice()

        if UPTO >= 2:
            P.barrier()
            st["off"] = SSM_KEEP
            tmpA = alloc([128, 4, NB], F32, "tmpA2")
            tmpB = alloc([128, 4, NB], F32, "tmpB2")
            sqt = [alloc([128, NB], F32, "sqt0")] * 2
            Wread = alloc([128, 4, 17, 2, 128], BF16, "Wread", nb=34)
            XB = alloc([128, 4, 2, 128], BF16, "XB")
            Winj = alloc([128, 4, TB, 2, 128], BF16, "Winj", nb=2 * TB)
            Kconv = alloc([128, TB, 128], BF16, "Kconv")
            uT = [alloc([128, L], BF16, f"uT{i}") for i in range(2)]
            uD = alloc([128, TB, NB], BF16, "uD", nb=2)
            yg = [alloc([128, L], BF16, "ygt0")] * 2
            mr = alloc([128, 4, NB], F32, "mr")
            mi = alloc([128, 4, NB], F32, "mi")
            wr = alloc([128, 4, NB], F32, "wr")
            wi = alloc([128, 4, NB], F32, "wi")
            Xp = alloc([128, 4, 2, NB + 1], BF16, "Xp")
            ytmp = [alloc([128, NB], F32, f"ytmp{i}") for i in range(2)]
            gt = [alloc([128, NB], F32, f"gt{i}") for i in range(2)]
            mset("pool", Xp[:], 0.0, [Xp])
            print("P2a arena", st["off"], flush=True)
            it = 0
            for ct in range(4):
                m8 = maskS[:].rearrange("p g a b -> p g (a b)").unsqueeze(3).to_broadcast([128, 4, 8, 16])
                for k in range(17):
                    for ri in range(2):
                        eng = "pool" if (k + ri) % 2 else "dve"
                        tt(eng, Wread[:, :, k, ri, :].rearrange("p g (m c) -> p g m c", c=16), m8,
                           VR[:, k, ri, 4 * ct:4 * ct + 4, :].unsqueeze(2).to_broadcast([128, 4, 8, 16]), ALU.mult,
                           [maskS, VR], [Wread.b[2 * k + ri]])
                for ri in range(2):
                    tt("dve", XB[:, :, ri, :].rearrange("p g (m c) -> p g m c", c=16), m8,
                       VB[:, ri, 4 * ct:4 * ct + 4, :].unsqueeze(2).to_broadcast([128, 4, 8, 16]), ALU.mult,
                       [maskS, VB], [XB])
                mC = maskC[:].unsqueeze(3).to_broadcast([128, 4, 2, 64])
                for j in range(TB):
                    for ri in range(2):
                        eng = "pool" if (j + ri) % 2 else "dve"
                        tt(eng, Winj[:, :, j, ri, :].rearrange("p g (a q) -> p g a q", q=64), mC,
                           VI[:, j, ri, ct * 64:(ct + 1) * 64].unsqueeze(1).unsqueeze(1).to_broadcast([128, 4, 2, 64]),
                           ALU.mult, [maskC, VI], [Winj.b[2 * j + ri]])
                for t4 in range(TB // 4):
                    pb = t4 % 2
                    for tq in range(4):
                        tau = t4 * 4 + tq
                        i = 0
                        for g in range(4):
                            for ri in range(2):
                                mm(ps[pb][:, tq * 128:(tq + 1) * 128], XB[:, g, ri, :], Wread[:, g, tau, ri, :], i == 0, i == 7,
                                   [XB, Wread.b[2 * tau + ri]], [b_ps[pb]])
                                i += 1
                    cp("act", Kconv[:, t4 * 4:(t4 + 1) * 4, :].rearrange("p a b -> p (a b)"), ps[pb][:], [b_ps[pb]], [Kconv])
                if ct == 0:
                    dump("Wread", Wread, [128, 4, 17, 2, 128], BF16)
                    dump("Winj", Winj, [128, 4, TB, 2, 128], BF16)
                    dump("XB", XB, [128, 4, 2, 128], BF16)
                    dump("Kconv", Kconv, [128, TB, 128], BF16)
                for s in range(NSEQ):
                    u_, y_ = uT[it % 2], yg[it % 2]
                    if it == 0:
                        dma("sp", u_[:], uTv[:, ct, s * L:(s + 1) * L], [b_uT[(s, p0)] for (p0, n) in tiles], [u_], u_)
                    it += 1
                    if it < 4 * NSEQ:
                        ctn, sn_ = it // NSEQ, it % NSEQ
                        un = uT[it % 2]
                        dma("sp", un[:], uTv[:, ctn, sn_ * L:(sn_ + 1) * L], [b_uT[(sn_, p0)] for (p0, n) in tiles], [un], un)
                    ub_ = u_[:].rearrange("p (b j) -> p b j", j=TB)
                    yb_ = y_[:].rearrange("p (b j) -> p b j", j=TB)
                    cp("act", uD[:, 0:TB // 2, :], u_[:].rearrange("p (b j) -> p j b", j=TB)[:, 0:TB // 2, :], [u_], [uD.b[0]])
                    cp("pool", uD[:, TB // 2:TB, :], u_[:].rearrange("p (b j) -> p j b", j=TB)[:, TB // 2:TB, :], [u_], [uD.b[1]])
                    for g in range(4):
                        for ri in range(2):
                            pp = g * 2 + ri
                            for j in range(TB):
                                mm(ps[pp][:, :NB], Winj[:, g, j, ri, :], uD[:, j, :], j == 0, j == TB - 1,
                                   [Winj.b[2 * j + ri], uD], [b_ps[pp]])
                    cc, ss = Tr[:, 4 * ct:4 * ct + 4, :], Ti[:, 4 * ct:4 * ct + 4, :]
                    pv = psall[:].rearrange("p (g r c) -> p g r c", r=2, c=512)
                    ir_, ii_ = pv[:, :, 0, 0:NB], pv[:, :, 1, 0:NB]
                    bre = [b_ps[0], b_ps[2], b_ps[4], b_ps[6]]
                    bim = [b_ps[1], b_ps[3], b_ps[5], b_ps[7]]
                    ta, tb = tmpA[:], tmpB[:]
                    tt("dve", ta, cc, ir_, ALU.mult, [Tr] + bre, [tmpA])
                    tt("dve", tb, ss, ii_, ALU.mult, [Ti] + bim, [tmpB])
                    tt("dve", mr[:], ta, tb, ALU.add, [tmpA, tmpB], [mr])
                    tt("dve", ta, cc, ii_, ALU.mult, [Tr] + bim, [tmpA])
                    tt("dve", tb, ss, ir_, ALU.mult, [Ti] + bre, [tmpB])
                    tt("dve", mi[:], ta, tb, ALU.subtract, [tmpA, tmpB], [mi])
                    for g in range(4):
                        rb = r16[:, 4 * ct + g:4 * ct + g + 1].to_broadcast([128, NB])
                        P.op("dve", lambda e, g=g, rb=rb: e.tensor_tensor_scan(out=wr[:, g, :], data0=rb, data1=mr[:, g, :],
                                                                              initial=0.0, op0=ALU.mult, op1=ALU.add),
                             bl([r16, mr]), bl([wr]))
                        P.op("dve", lambda e, g=g, rb=rb: e.tensor_tensor_scan(out=wi[:, g, :], data0=rb, data1=mi[:, g, :],
                                                                              initial=0.0, op0=ALU.mult, op1=ALU.add),
                             bl([r16, mi]), bl([wi]))
                    tt("dve", ta, cc, wr[:], ALU.mult, [Tr, wr], [tmpA])
                    tt("dve", tb, ss, wi[:], ALU.mult, [Ti, wi], [tmpB])
                    tt("dve", Xp[:, :, 0, 1:NB + 1], ta, tb, ALU.subtract, [tmpA, tmpB], [Xp])
                    tt("pool", mr[:], cc, wi[:], ALU.mult, [Tr, wi], [mr])
                    tt("pool", mi[:], ss, wr[:], ALU.mult, [Ti, wr], [mi])
                    tt("pool", Xp[:, :, 1, 1:NB + 1], mr[:], mi[:], ALU.add, [mr, mi], [Xp])
                    if ct == 0 and s == 0:
                        dump("mr", mr, [128, 4, NB], F32)
                        dump("mi", mi, [128, 4, NB], F32)
                        dump("wr", wr, [128, 4, NB], F32)
                        dump("Xp", Xp, [128, 4, 2, NB + 1], BF16)
                    for j in range(TB):
                        pp = j % 2
                        nmm = 8 + j + 1
                        i = 0
                        for g in range(4):
                            for ri in range(2):
                                mm(ps[pp][:, :NB], Wread[:, g, j + 1, ri, :], Xp[:, g, ri, 0:NB], i == 0, i == nmm - 1,
                                   [Wread.b[2 * (j + 1) + ri], Xp], [b_ps[pp]])
                                i += 1
                        for tau in range(j + 1):
                            mm(ps[pp][:, :NB], Kconv[:, tau, :], uD[:, j - tau, :], i == 0, i == nmm - 1,
                               [Kconv, uD], [b_ps[pp]])
                            i += 1
                        yt, g_, sq_ = ytmp[pp], gt[pp], sqt[pp]
                        stt("dve", yt[:], uD[:, j, :], dC[:, ct:ct + 1], ps[pp][:, :NB], ALU.mult, ALU.add,
                            [uD, dC, b_ps[pp]], [yt])
                        actf(sq_[:], yt[:], AF.Square, [yt], [sq_])
                        ts("pool", g_[:], sq_[:], 0.044715, 1.0, ALU.mult, ALU.add, [sq_], [g_])
                        tt("pool", g_[:], g_[:], yt[:], ALU.mult, [g_, yt], [g_])
                        actf(g_[:], g_[:], AF.Sigmoid, [g_], [g_], scale=GELU_C)
                        tt("dve", yb_[:, :, j], yt[:], g_[:], ALU.mult, [yt, g_], [y_])
                    dma("sp", ygv[:, ct, s * L:(s + 1) * L], y_[:], [y_], [b_yg[s]], y_)

        if UPTO >= 3:
            phase_begin()
            cs2 = [ffn_alloc(with_weights=False) for _ in range(2)]
            Wout = alloc([128, 4, 2 * D], BF16, "Wout")
            ygb = [alloc([128, 4, 512], BF16, f"ygb{i}") for i in range(2)]
            dma("pool", Wout[:], w_out.rearrange("(c p) f -> p c f", p=128), [], [Wout], Wout)
            it = 0
            for s in range(NSEQ):
                for (p0, n) in tiles:
                    yb = ygb[it % 2]
                    c = cs2[it % 2]
                    it += 1
                    load_h(c, s, p0, n)
                    dma("sp", yb[:, :, :n], ygv[:, :, s * L + p0:s * L + p0 + n], [b_yg[s]], [yb], yb)
                    for k in range(KC):
                        pa, pg = 2 * (k % 2), 2 * (k % 2) + 1
                        for ct in range(4):
                            mm(ps[pa][:, :n], Wout[:, ct, k * 128:(k + 1) * 128], yb[:, ct, :n], ct == 0, ct == 3,
                               [Wout, yb], [b_ps[pa]])
                        for ct in range(4):
                            mm(ps[pg][:, :n], Wout[:, ct, D + k * 128:D + (k + 1) * 128], yb[:, ct, :n], ct == 0, ct == 3,
                               [Wout, yb], [b_ps[pg]])
                        sgk = c.sg[k % 2]
                        actf(sgk[:, :n], ps[pg][:, :n], AF.Sigmoid, [b_ps[pg]], [sgk])
                        tt("dve", sgk[:, :n], sgk[:, :n], ps[pa][:, :n], ALU.mult, [sgk, b_ps[pa]], [sgk])
                        tt("pool", c.h[:, k, :n], c.h[:, k, :n], sgk[:, :n], ALU.add, [c.h.b[k], sgk], [c.h.b[k]])
                    store_h(c, s, p0, n)

        def qk_part1(n, src_ps, b_src, gcol, tset, psA):
            sqh, qh, rs, t1, t2 = tset
            actf(sqh[:, :n], src_ps[:, :n], AF.Square, [b_src], [sqh])
            mm(ps[psA][:, :n], ones2[:], sqh[:, :n], True, True, [ones2, sqh], [b_ps[psA]])
            actf(rs[:, :n], ps[psA][:, :n], AF.Ln, [b_ps[psA]], [rs], scale=1.0 / HD, bias=EPS)
            actf(rs[:, :n], rs[:, :n], AF.Exp, [rs], [rs], scale=-0.5)
            stt("dve", t1[:, :n], src_ps[:, :n], gcol, rs[:, :n], ALU.mult, ALU.mult, [b_src, qkg, rs], [t1])
            cp("act", qh[:, :n], t1[:, :n], [t1], [qh])

        def qk_part2(n, cs, dst, W, tset, psB):
            sqh, qh, rs, t1, t2 = tset
            mm(ps[psB][:, :n], rmat[:], qh[:, :n], True, True, [rmat, qh], [b_ps[psB]])
            tt("pool", t1[:, :n], t1[:, :n], cs[:, 0, :n], ALU.mult, [t1, cs], [t1])
            tt("dve", t2[:, :n], ps[psB][:, :n], cs[:, 1, :n], ALU.mult, [b_ps[psB], cs], [t2])
            tt("pool", dst, t1[:, :n], t2[:, :n], ALU.add, [t1, t2], W)

        def qk_norm_rope(n, src_ps, b_src, gcol, cs, sn, dst, W, tq, tr_, psA, psB):
            tset = (tq[0], tq[1], tr_[0], tr_[1], tr_[2])
            qk_part1(n, src_ps, b_src, gcol, tset, psA)
            qk_part2(n, cs, dst, W, tset, psB)

        ropev = rope_d.rearrange("c p t -> p c t")
        KTv = KTs.rearrange("(j p) t -> p j t", p=128)

        if UPTO >= 4:
            phase_begin()
            c = ffn_alloc()
            load_ffn_weights(c, 0, 1)
            ffn_phase_pipelined(c, 1, "mid")
            phase_begin()
            cs2c = [ffn_alloc(with_weights=False) for _ in range(2)]
            Wkv = alloc([128, KC, 512], BF16, "Wkv")
            dma("pool", Wkv[:], w_kv.rearrange("(k p) f -> p k f", p=128), [], [Wkv], Wkv)
            csb2 = [alloc([128, 2, 512], F32, f"csk{i}") for i in range(3)]
            ktps = [alloc([128, 2, 512], BF16, f"ktp{i}") for i in range(2)]
            vtps = [alloc([128, 4, 256], BF16, f"vtp{i}") for i in range(2)]
            tsk = [(alloc([128, 512], BF16, f"ksq{i}"), alloc([128, 512], BF16, f"kqh{i}"), alloc([128, 512], F32, f"krs{i}"),
                    alloc([128, 512], F32, f"kt1{i}"), alloc([128, 512], F32, f"kt2{i}")) for i in range(2)]
            tl2 = [(s, p0, n) for s in range(NSEQ) for (p0, n) in tiles]

            def pre2(i):
                s, p0, n = tl2[i]
                load_h(cs2c[i % 2], s, p0, n)
                dma("sp", csb2[i % 3][:, :, :n], ropev[:, :, p0:p0 + n], [], [csb2[i % 3]], csb2[i % 3])

            pre2(0)
            rmsnorm(cs2c[0], tl2[0][2], 6)

            def kv_chain(i):
                s, p0, n = tl2[i]
                cs, ktp, vtp = csb2[i % 3], ktps[i % 2], vtps[i % 2]
                PBk = (0, 1) if i % 2 == 0 else (2, 3)
                for j in range(2):
                    qk_part1(n, ps[PBk[j]], b_ps[PBk[j]], qkg[:, 1:2], tsk[j], 7)
                for j in range(2):
                    qk_part2(n, cs, ktp[:, j, :n], [ktp], tsk[j], PBk[j])
                dma("pool", KTv[:, :, s * L + p0:s * L + p0 + n], ktp[:, :, :n], [ktp], [b_kv[(s, p0)]], ktp)
                r0 = s * L + p0
                if n == 512:
                    dma("pool", Vs[r0:r0 + n, :].rearrange("(b p) f -> p b f", p=128), vtp[:, :, :], [vtp], [b_kv[(s, p0)]], vtp)
                else:
                    dma("pool", Vs[r0:r0 + n, :], vtp[:n, 0, :], [vtp], [b_kv[(s, p0)]], vtp)

            for i, (s, p0, n) in enumerate(tl2):
                c = cs2c[i % 2]
                vtp = vtps[i % 2]
                PBk = (0, 1) if i % 2 == 0 else (2, 3)
                if i + 1 < len(tl2):
                    pre2(i + 1)
                for j in range(2):
                    for k in range(KC):
                        mm(ps[PBk[j]][:, :n], Wkv[:, k, j * 128:(j + 1) * 128], c.hn[:, k, :n], k == 0, k == KC - 1,
                           [Wkv, c.hn.b[k]], [b_ps[PBk[j]]])
                nb_ = (n + 127) // 128
                for bi in range(nb_):
                    m = min(128, n - bi * 128)
                    pv_ = 4 + (bi % 2)
                    for k in range(KC):
                        mm(ps[pv_][:m, 0:256], c.hn[:, k, bi * 128:bi * 128 + m], Wkv[:, k, 256:512], k == 0, k == KC - 1,
                           [Wkv, c.hn.b[k]], [b_ps[pv_]])
                    cp("act" if bi % 2 == 0 else "dve", vtp[:m, bi, :], ps[pv_][:m, 0:256], [b_ps[pv_]], [vtp])
                if i + 1 < len(tl2):
                    rmsnorm(cs2c[(i + 1) % 2], tl2[i + 1][2], 6)
                if i >= 1:
                    kv_chain(i - 1)
            kv_chain(len(tl2) - 1)

        if UPTO >= 5:
            phase_begin()
            c = ffn_alloc()
            load_ffn_weights(c, 1, 0)
            ffn_phase_pipelined(c, 2, "q")

        if UPTO >= 6:
            phase_begin()
            cs4 = [ffn_alloc(with_weights=False) for _ in range(2)]
            Wq = alloc([128, KC, D], BF16, "Wq")
            Wo = alloc([128, KC, D], BF16, "Wo")
            dma("pool", Wq[:], w_q.rearrange("(k p) f -> p k f", p=128), [], [Wq], Wq)
            dma("pool", Wo[:], w_o.rearrange("(k p) f -> p k f", p=128), [], [Wo], Wo)
            csb = [alloc([128, 2, 512], F32, f"cs{i}") for i in range(2)]
            tsets = [(alloc([128, 512], BF16, f"sqh{i}"), alloc([128, 512], BF16, f"qh{i}"), alloc([128, 512], F32, f"rs{i}"),
                      alloc([128, 512], F32, f"t1{i}"), alloc([128, 512], F32, f"t2{i}")) for i in range(2)]
            Qp = alloc([128, 8, 512], BF16, "Qp", nb=8)
            oT = alloc([128, 8, 512], BF16, "oT", nb=8)
            Kdb = [alloc([128, 4, 640], BF16, f"Kd{i}") for i in range(2)]
            Kmb = [alloc([128, 4, NMETA], BF16, f"Km{i}") for i in range(2)]
            Vlob = [alloc([128, 5, 4, 128], BF16, f"Vlo{i}", nb=5) for i in range(2)]
            Vhib = [alloc([128, 5, 4, 128], BF16, f"Vhi{i}", nb=5) for i in range(2)]
            Vmlob = [alloc([NMETA, 4, 128], BF16, f"Vmlo{i}") for i in range(2)]
            Vmhib = [alloc([NMETA, 4, 128], BF16, f"Vmhi{i}") for i in range(2)]
            olo = alloc([128, 128], BF16, "olo")
            ohi = alloc([128, 128], BF16, "ohi")
            PTp = [alloc([128, 512], BF16, f"PTp{i}") for i in range(2)]
            PTc = [alloc([128, 512], BF16, f"PTc{i}") for i in range(2)]
            PTm = [alloc([128, 512], BF16, f"PTm{i}") for i in range(2)]
            mask2 = alloc([128, 2, 128], BF16, "mask2")
            rden = [alloc([128, 512], F32, f"rden{i}") for i in range(2)]
            for t_ in Vlob + Vhib + Vmlob + Vmhib + [olo, ohi]:
                mset("pool", t_[:], 0.0, [t_])
            mset("pool", olo[:, 0:64], 1.0, [olo])
            mset("pool", ohi[:, 64:128], 1.0, [ohi])
            mi32 = alloc([128, 128], mybir.dt.int32, "mi32")
            mf32 = alloc([128, 128], F32, "mf32")
            P.op("pool", lambda e: e.iota(mi32[:], pattern=[[1, 128]], base=0, channel_multiplier=-1), [], bl([mi32]))
            cp("dve", mf32[:], mi32[:], [mi32], [mf32])
            ts("dve", mask2[:, 1, :], mf32[:], 0.0, None, ALU.is_ge, None, [mf32], [mask2])
            ts("dve", mask2[:, 0, :], mf32[:], 0.0, None, ALU.is_lt, None, [mf32], [mask2])
            scale = HD ** -0.5
            SCB = (0, 1)
            SMB = (2, 3)
            OB = (4, 6)
            DB = (5, 7)
            tl4 = [(s, ti, p0, n) for s in range(NSEQ) for ti, (p0, n) in enumerate(qtiles)]

            def pre_dma(i):
                s, ti, p0, n = tl4[i]
                c = cs4[i % 2]
                if ti == 0:
                    Km, Vmlo, Vmhi = Kmb[s % 2], Vmlob[s % 2], Vmhib[s % 2]
                    for hh in range(2):
                        dma("sp", Km[64 * hh:64 * hh + 64, :, :], KTs[:, s * L:s * L + NMETA].rearrange("(h d) t -> d h t", d=64),
                            [b_kv[(s, 0)]], [Km], Km)
                    dma("sp", Vmlo[:, :, 0:64], Vs[s * L:s * L + NMETA, :].rearrange("t (h d) -> t h d", d=64), [b_kv[(s, 0)]], [Vmlo], Vmlo)
                    dma("sp", Vmhi[:, :, 64:128], Vs[s * L:s * L + NMETA, :].rearrange("t (h d) -> t h d", d=64), [b_kv[(s, 0)]], [Vmhi], Vmhi)
                load_h(c, s, p0, n)
                cs = csb[i % 2]
                dma("sp", cs[:, :, :n], ropev[:, :, p0:p0 + n], [], [cs], cs)
                Kd, Vlo, Vhi = Kdb[i % 2], Vlob[i % 2], Vhib[i % 2]
                k0 = p0 - 128 if ti > 0 else p0
                nk = p0 + n - k0
                ko = 0 if ti > 0 else 1
                kvdeps = [b_kv[(s, p0)]] + ([b_kv[(s, qtiles[ti - 1][0])]] if ti > 0 else [])
                for hh in range(2):
                    dma("sp", Kd[64 * hh:64 * hh + 64, :, ko * 128:ko * 128 + nk],
                        KTs[:, s * L + k0:s * L + k0 + nk].rearrange("(h d) t -> d h t", d=64), kvdeps, [Kd], Kd)
                for bi in range(nk // 128):
                    vsrc = Vs[s * L + k0 + bi * 128:s * L + k0 + (bi + 1) * 128, :].rearrange("p (h d) -> p h d", d=64)
                    dma("sp", Vlo[:, ko + bi, :, 0:64], vsrc, kvdeps, [Vlo.b[ko + bi]], Vlo.b[ko + bi])
                    dma("sp", Vhi[:, ko + bi, :, 64:128], vsrc, kvdeps, [Vhi.b[ko + bi]], Vhi.b[ko + bi])

            def pre_norm(i):
                s, ti, p0, n = tl4[i]
                rmsnorm(cs4[i % 2], n, 5)

            pre_dma(0)
            pre_norm(0)
            for it4, (s, ti, p0, n) in enumerate(tl4):
                if True:
                    c = cs4[it4 % 2]
                    cs = csb[it4 % 2]
                    Kd, Vlo, Vhi = Kdb[it4 % 2], Vlob[it4 % 2], Vhib[it4 % 2]
                    Km, Vmlo, Vmhi = Kmb[s % 2], Vmlob[s % 2], Vmhib[s % 2]
                    if it4 + 1 < len(tl4):
                        pre_dma(it4 + 1)
                    QB = (0, 1, 2)
                    SB_ = (3, 4)
                    RB = (5, 6)
                    for jj in range(8 + 2):
                        if jj < 8:
                            j = jj
                            pq = QB[j % 3]
                            for k in range(KC):
                                mm(ps[pq][:, :n], Wq[:, k, j * 128:(j + 1) * 128], c.hn[:, k, :n], k == 0, k == KC - 1,
                                   [Wq, c.hn.b[k]], [b_ps[pq]])
                        if 0 <= jj - 1 < 8:
                            j = jj - 1
                            qk_part1(n, ps[QB[j % 3]], b_ps[QB[j % 3]], qkg[:, 0:1], tsets[j % 2], SB_[j % 2])
                        if 0 <= jj - 2 < 8:
                            j = jj - 2
                            qk_part2(n, cs, Qp[:, j, :n], [Qp.b[j]], tsets[j % 2], RB[j % 2])
                    if it4 + 1 < len(tl4):
                        pre_norm(it4 + 1)
                    steps = [(kvh, blk) for kvh in range(4) for blk in range(4)]
                    PB = (0, 3)
                    CB = (1, 4)
                    MB = (2, 5)
                    OB = (6, 7)

                    def emit_scores(i):
                        kvh, blk = steps[i]
                        pb_, cb_, mb_ = PB[i % 2], CB[i % 2], MB[i % 2]
                        qs = slice(blk * 128, (blk + 1) * 128)
                        has_prev = not (ti == 0 and blk == 0)
                        qb = [Qp.b[2 * kvh], Qp.b[2 * kvh + 1]]
                        for eo in range(2):
                            rows = slice(64 * eo, 64 * eo + 64)
                            cols = slice(eo * 256, (eo + 1) * 256)
                            qv = Qp[rows, 2 * kvh:2 * kvh + 2, qs]
                            if has_prev:
                                mm(ps[pb_][:, cols], Kd[rows, kvh, blk * 128:(blk + 1) * 128], qv, True, True, [Kd] + qb, [b_ps[pb_]])
                            mm(ps[cb_][:, cols], Kd[rows, kvh, (blk + 1) * 128:(blk + 2) * 128], qv, True, True, [Kd] + qb, [b_ps[cb_]])
                            mm(ps[mb_][:NMETA, cols], Km[rows, kvh, :], qv, True, True, [Km] + qb, [b_ps[mb_]])

                    def emit_rest(i):
                        kvh, blk = steps[i]
                        pb_, cb_, mb_ = PB[i % 2], CB[i % 2], MB[i % 2]
                        ptp, ptc, ptm = PTp[i % 2], PTc[i % 2], PTm[i % 2]
                        ob = OB[i % 2]
                        has_prev = not (ti == 0 and blk == 0)
                        e1, e2 = ("dve", "pool") if i % 2 == 0 else ("pool", "dve")
                        if has_prev:
                            actf(ptp[:], ps[pb_][:], AF.Exp, [b_ps[pb_]], [ptp], scale=scale)
                            v3 = ptp[:].rearrange("p (a q) -> p a q", q=128)
                            tt(e1, v3, v3, mask2[:, 0, :].unsqueeze(1).to_broadcast([128, 4, 128]), ALU.mult, [ptp, mask2], [ptp])
                        actf(ptc[:], ps[cb_][:], AF.Exp, [b_ps[cb_]], [ptc], scale=scale)
                        v3 = ptc[:].rearrange("p (a q) -> p a q", q=128)
                        tt(e2, v3, v3, mask2[:, 1, :].unsqueeze(1).to_broadcast([128, 4, 128]), ALU.mult, [ptc, mask2], [ptc])
                        actf(ptm[:NMETA, :], ps[mb_][:NMETA, :], AF.Exp, [b_ps[mb_]], [ptm], scale=scale)
                        seqm = []
                        for eo in range(2):
                            cols = slice(eo * 256, (eo + 1) * 256)
                            V_ = Vlo if eo == 0 else Vhi
                            Vm_ = Vmlo if eo == 0 else Vmhi
                            o_ = olo if eo == 0 else ohi
                            if has_prev:
                                seqm.append((V_[:, blk, kvh, :], o_[:], ptp[:, cols], [V_.b[blk], o_, ptp]))
                            seqm.append((V_[:, blk + 1, kvh, :], o_[:], ptc[:, cols], [V_.b[blk + 1], o_, ptc]))
                            seqm.append((Vm_[:, kvh, :], o_[:NMETA, :], ptm[:NMETA, cols], [Vm_, o_, ptm]))
                        for ii, (vv, oo, pp_, R) in enumerate(seqm):
                            mm(ps[ob][:, 0:256], vv, pp_, ii == 0, ii == len(seqm) - 1, R, [b_ps[ob]])
                        for ii, (vv, oo, pp_, R) in enumerate(seqm):
                            mm(ps[ob][:, 256:512], oo, pp_, ii == 0, ii == len(seqm) - 1, R, [b_ps[ob]])
                        pending.append((i,))

                    def finalize(i):
                        kvh, blk = steps[i]
                        ob = OB[i % 2]
                        rd = rden[i % 2]
                        qs = slice(blk * 128, (blk + 1) * 128)
                        for pr in range(2):
                            ts("dve", rd[:, pr * 128:(pr + 1) * 128], ps[ob][:, 256 + pr * 128:256 + (pr + 1) * 128],
                               sinkx[:, 2 * kvh + pr:2 * kvh + pr + 1], None, ALU.add, None, [b_ps[ob], sinkx], [rd])
                        actf(rd[:, 0:256], rd[:, 0:256], AF.Ln, [rd], [rd])
                        actf(rd[:, 0:256], rd[:, 0:256], AF.Exp, [rd], [rd], scale=-1.0)
                        tt("dve", oT[:, 2 * kvh:2 * kvh + 2, qs], ps[ob][:, 0:256].rearrange("p (a q) -> p a q", q=128),
                           rd[:, 0:256].rearrange("p (a q) -> p a q", q=128), ALU.mult, [b_ps[ob], rd],
                           [oT.b[2 * kvh], oT.b[2 * kvh + 1]])

                    pending = []
                    emit_scores(0)
                    for i in range(len(steps)):
                        if i + 1 < len(steps):
                            emit_scores(i + 1)
                        fin = list(pending)
                        del pending[:]
                        emit_rest(i)
                        for f_ in fin:
                            finalize(*f_)
                    for f_ in pending:
                        finalize(*f_)
                    for k in range(KC):
                        po = k % 2
                        for j in range(8):
                            mm(ps[po][:, :n], Wo[:, j, k * 128:(k + 1) * 128], oT[:, j, :n], j == 0, j == 7,
                               [Wo, oT.b[j]], [b_ps[po]])
                        tt("dve", c.h[:, k, :n], c.h[:, k, :n], ps[po][:, :n], ALU.add, [c.h.b[k], b_ps[po]], [c.h.b[k]])
                    store_h(c, s, p0, n)

        if UPTO >= 7:
            phase_begin()
            c = ffn_alloc()
            load_ffn_weights(c, 1, 1)
            ffn_phase_pipelined(c, 3, "out")

        dbg = {}
        if 2 <= UPTO < 7:
            P.barrier()
            for nm, src, shp, dt in (("d_hA", hA, [D, LT], F32), ("d_yg", ygs, [512, LT], BF16), ("d_uT", uTs, [512, LT], BF16),
                                     ("d_KT", KTs, [256, LT], BF16), ("d_V", Vs, [LT, 256], BF16)):
                dd = nc.dram_tensor(nm, shp, dt, kind="ExternalOutput").ap()
                bo = P.buf(nm)
                dma("sp", dd, src, [], [bo], bo)
                b_out.append(bo)
        P.finalize(es, b_out + dbg_bufs)
        print("prog stats", P.stats, "arena", st["off"], flush=True)
    return nc


def _rope_tables(L):
    half = HD // 2
    freqs = (np.float32(10000.0) ** (-np.arange(0, half, dtype=np.float32) * np.float32(2.0) / np.float32(HD))).astype(np.float32)
    ang = np.arange(L, dtype=np.float32)[None, :] * freqs[:, None]
    cos = np.cos(ang).astype(np.float32)
    sin = np.sin(ang).astype(np.float32)
    return np.ascontiguousarray(np.stack([np.tile(cos, (4, 1)), np.tile(sin, (4, 1))], 0))


def _rot_matrix():
    R = np.zeros((128, 128), np.float32)
    for hb in (0, 64):
        for dp in range(32):
            R[hb + dp + 32, hb + dp] = -1.0
            R[hb + dp, hb + dp + 32] = 1.0
    return R


def prep_shared(inp, L):
    f = lambda a: np.ascontiguousarray(np.asarray(a, dtype=np.float32))
    m = {}
    m["metaT"] = f(np.asarray(inp["meta_tokens"]).T)
    for l in range(2):
        m[f"wgu{l}0"] = f(inp["ffn1_w_gate_up"][l])
        m[f"wdn{l}0"] = f(inp["ffn1_w_down"][l])
        m[f"wgu{l}1"] = f(inp["ffn2_w_gate_up"][l])
        m[f"wdn{l}1"] = f(inp["ffn2_w_down"][l])
    g = np.stack([inp["ffn1_norm"][0], inp["ffn2_norm"][0], inp["ffn1_norm"][1], inp["ffn2_norm"][1],
                  inp["mix_norm"][0], inp["mix_norm"][1], inp["kv_norm"]], 0)
    m["gains"] = f(np.asarray(g).reshape(7, KC, 128).transpose(2, 0, 1))
    m["w_in"] = f(inp["ssm_w_in"][0])
    m["w_out"] = f(inp["ssm_w_out"][0])
    m["w_kv"] = f(inp["w_kv"])
    m["w_q"] = f(inp["attn_w_q"][0])
    m["w_o"] = f(inp["attn_w_o"][0])
    lr = np.asarray(inp["ssm_lambda_re"][0]); li = np.asarray(inp["ssm_lambda_im"][0])
    ls = np.broadcast_to(np.asarray(inp["ssm_log_step"][0])[:, None], (NG, NP))
    toS = lambda a: np.asarray(a).reshape(16, 2, NP).transpose(1, 2, 0).reshape(128, 16)
    m["lamS"] = f(np.stack([toS(lr), toS(li), toS(ls)], 1))
    br = np.asarray(inp["ssm_b_re"][0]); bi = np.asarray(inp["ssm_b_im"][0])
    cr = np.asarray(inp["ssm_c_re"][0]); ci = np.asarray(inp["ssm_c_im"][0])
    bS = lambda a: a.reshape(16, 2, NP, 16).transpose(1, 2, 0, 3).reshape(128, 256)
    cS = lambda a: a.reshape(16, 2, 16, NP).transpose(1, 3, 0, 2).reshape(128, 256)
    m["bcS"] = f(np.stack([bS(br), bS(bi), cS(cr), cS(ci)], 1))
    q = np.arange(128)
    gidx = (8 * np.arange(4)[None, :] + (q // 16)[:, None])
    toC = lambda a: np.asarray(a)[gidx].reshape(128, 256)
    m["lamC"] = f(np.stack([toC(lr), toC(li), toC(ls)], 1))
    bCf = lambda a: a[gidx, :, (q % 16)[:, None]].reshape(128, 256)
    m["bC"] = f(np.stack([bCf(br), bCf(bi)], 1))
    m["dC"] = f(np.asarray(inp["ssm_d"][0]).reshape(4, 128).T)
    m["qkg"] = f(np.stack([np.tile(np.asarray(inp["q_norm"][0]), 2), np.tile(np.asarray(inp["k_norm"]), 2)], 1))
    sk = np.asarray(inp["attn_sinks"][0])
    m["sinkL"] = f(sk.reshape(8, 2)[:, (q // 64)].T)
    m["ropeT"] = _rope_tables(L)
    m["rmat"] = _rot_matrix()
    return m


_NC = {}


def kernel(**inputs):
    x = np.asarray(inputs["x"], dtype=np.float32)
    B, S, _ = x.shape
    ncores = 8
    nseq = B // ncores
    key = (S, nseq)
    if key not in _NC:
        _NC[key] = build(Cfg(seq=S, nseq=nseq))
    nc = _NC[key]
    shared = prep_shared(inputs, S + NMETA)
    in_maps = []
    for c in range(ncores):
        m = dict(shared)
        m["xT"] = np.ascontiguousarray(x[c * nseq:(c + 1) * nseq].reshape(nseq * S, D).T)
        in_maps.append(m)
    res = run_bass_kernel_spmd(nc, in_maps, core_ids=list(range(ncores)))
    out = np.empty((B, S, D), np.float32)
    for c in range(ncores):
        out[c * nseq:(c + 1) * nseq] = res.results[c]["outT"].T.reshape(nseq, S, D)
    return out
```
